# Optimizing a Trainium2 kernel written in Bass

```python
import math
import jax, jax.numpy as jnp
from jax import lax
import numpy as np

D_MODEL = 1024
BATCH = 8
SEQ = 8192
DEPTH = 1
DEC_BATCH = 32
DEC_SEQ = 64
PAST_LEN = 1024

CHUNK = 64
SB_HEADS = 8
SB_HEAD_DIM = 128
SB_WIDTH = SB_HEADS * SB_HEAD_DIM
SB_BLOCK = 128
LRU_WIDTH = 1024
LRU_BLOCKS = 8
LRU_BLOCK_DIM = LRU_WIDTH // LRU_BLOCKS
CONV_WIDTH = 4
LRU_C = 8.0
MEM_TOKENS = 256
MEM_HEADS = 4
MEM_HEAD_DIM = 256
MEM_WIDTH = MEM_HEADS * MEM_HEAD_DIM
N_BRANCH = 3
FFN_HIDDEN = -(-8 * D_MODEL // (3 * 256)) * 256
IN_WIDTH = 3 * SB_WIDTH + 2 * LRU_WIDTH + MEM_WIDTH + N_BRANCH * D_MODEL
EPS = 1e-6

kernel_name = "stickbreak_rglru_memxattn_encoder_step"


def rms_norm(x, g):
    xf = x.astype(jnp.float32)
    y = xf * lax.rsqrt(jnp.mean(xf * xf, axis=-1, keepdims=True) + EPS)
    return (y * g.astype(jnp.float32)).astype(x.dtype)


def sb_block(q, k, v, q_pos, k_pos):
    z = jnp.einsum("bhqd,bhkd->bhqk", q.astype(jnp.float32), k.astype(jnp.float32)) / math.sqrt(SB_HEAD_DIM)
    mask = k_pos[None, :] < q_pos[:, None]
    log_beta = jax.nn.log_sigmoid(z)
    log_surv = jnp.where(mask, jax.nn.log_sigmoid(-z), 0.0)
    shifted = jnp.pad(log_surv[..., 1:], ((0, 0), (0, 0), (0, 0), (0, 1)))
    after = lax.cumsum(shifted, axis=3, reverse=True)
    a = jnp.where(mask, jnp.exp(log_beta + after), 0.0)
    return jnp.einsum("bhqk,bhkd->bhqd", a, v.astype(jnp.float32))


def stick_breaking_attention(q, k, v, q_pos, k_pos):
    b, h, tq, d = q.shape
    if tq > SB_BLOCK and tq % SB_BLOCK == 0:
        nb = tq // SB_BLOCK
        qb = q.reshape(b, h, nb, SB_BLOCK, d).transpose(2, 0, 1, 3, 4)
        pb = q_pos.reshape(nb, SB_BLOCK)
        ob = lax.map(lambda qp: sb_block(qp[0], k, v, qp[1], k_pos), (qb, pb))
        return ob.transpose(1, 2, 0, 3, 4).reshape(b, h, tq, v.shape[-1])
    return sb_block(q, k, v, q_pos, k_pos)


def causal_conv(x, buf, w, bias):
    t = x.shape[1]
    xp = jnp.concatenate([buf.astype(x.dtype), x], axis=1)
    y = bias
    for i in range(CONV_WIDTH):
        y = y + xp[:, i:i + t] * w[i]
    return y, xp[:, -(CONV_WIDTH - 1):]


def block_diag_linear(x, w, bias):
    xb = x.reshape(x.shape[:-1] + (LRU_BLOCKS, LRU_BLOCK_DIM))
    y = jnp.einsum("btnd,nde->btne", xb, w.astype(x.dtype))
    return y.reshape(x.shape) + bias.astype(x.dtype)


def rg_lru(x, h0, wa, ba, wx, bx, a_logit):
    xf = x.astype(jnp.float32)
    r = jax.nn.sigmoid(block_diag_linear(xf, wa, ba))
    i = jax.nn.sigmoid(block_diag_linear(xf, wx, bx))
    log_a = LRU_C * r * jax.nn.log_sigmoid(a_logit.astype(jnp.float32))
    a = jnp.exp(log_a)
    bterm = jnp.sqrt(-jnp.expm1(2.0 * log_a)) * (i * xf)
    bterm = bterm.at[:, 0].add(a[:, 0] * h0.astype(jnp.float32))

    def combine(c1, c2):
        a1, b1 = c1
        a2, b2 = c2
        return a1 * a2, a2 * b1 + b2

    _, h = lax.associative_scan(combine, (a, bterm), axis=1)
    return h, h[:, -1]


def memory_kv(mem, g, wk, wv, k_norm_g):
    b, n, _ = mem.shape
    m = rms_norm(mem, g)
    k = rms_norm((m @ wk).reshape(b, n, MEM_HEADS, MEM_HEAD_DIM), k_norm_g)
    v = (m @ wv).reshape(b, n, MEM_HEADS, MEM_HEAD_DIM)
    return k.transpose(0, 2, 1, 3), v.transpose(0, 2, 1, 3)


def layer(x, past_k, past_v, conv_buf, h0, mem_k, mem_v, lw):
    b, t, _ = x.shape
    p = past_k.shape[2]
    xn = rms_norm(x, lw["norm_mix_g"])
    proj = xn @ lw["w_in"]
    splits = [SB_WIDTH, 2 * SB_WIDTH, 3 * SB_WIDTH, 3 * SB_WIDTH + LRU_WIDTH,
              3 * SB_WIDTH + 2 * LRU_WIDTH, 3 * SB_WIDTH + 2 * LRU_WIDTH + MEM_WIDTH]
    q_sb, k_sb, v_sb, x_lru, g_lru, q_mem, gates = jnp.split(proj, splits, axis=-1)

    heads = lambda z: z.reshape(b, t, SB_HEADS, SB_HEAD_DIM).transpose(0, 2, 1, 3)
    q_sb, k_new, v_new = heads(q_sb), heads(k_sb), heads(v_sb)
    k_all = jnp.concatenate([past_k.astype(k_new.dtype), k_new], axis=2)
    v_all = jnp.concatenate([past_v.astype(v_new.dtype), v_new], axis=2)
    q_pos = p + jnp.arange(t, dtype=jnp.int32)
    k_pos = jnp.arange(p + t, dtype=jnp.int32)
    o_sb = stick_breaking_attention(q_sb, k_all, v_all, q_pos, k_pos)
    o_sb = o_sb.transpose(0, 2, 1, 3).reshape(b, t, SB_WIDTH).astype(x.dtype)

    xc, new_buf = causal_conv(x_lru, conv_buf, lw["conv_w"], lw["conv_b"])
    h, h_last = rg_lru(xc, h0, lw["lru_wa"], lw["lru_ba"], lw["lru_wx"], lw["lru_bx"], lw["lru_a_logit"])
    o_lru = (h * jax.nn.gelu(g_lru.astype(jnp.float32), approximate=True)).astype(x.dtype)

    qm = rms_norm(q_mem.reshape(b, t, MEM_HEADS, MEM_HEAD_DIM), lw["q_norm_g"])
    s = jnp.einsum("bthd,bhnd->bhtn", qm.astype(jnp.float32), mem_k.astype(jnp.float32)) / math.sqrt(MEM_HEAD_DIM)
    pm = jax.nn.softmax(s, axis=-1)
    o_mem = jnp.einsum("bhtn,bhnd->bthd", pm, mem_v.astype(jnp.float32)).reshape(b, t, MEM_WIDTH).astype(x.dtype)

    g = jax.nn.sigmoid(gates.astype(jnp.float32) + lw["b_merge"].astype(jnp.float32)).reshape(b, t, N_BRANCH, D_MODEL)
    merged = (g[:, :, 0] * (o_sb @ lw["w_br_sb"]) + g[:, :, 1] * (o_lru @ lw["w_br_lru"])
              + g[:, :, 2] * (o_mem @ lw["w_br_mem"])).astype(x.dtype)
    x = x + merged @ lw["w_out"]

    xn2 = rms_norm(x, lw["norm_ffn_g"])
    f = (jax.nn.silu(xn2 @ lw["w_ffn_gate"]) * (xn2 @ lw["w_ffn_up"])) @ lw["w_ffn_down"]
    x = x + f
    return x, k_new, v_new, new_buf, h_last.astype(x.dtype)


def setup_inputs(seed: int = 0) -> dict:
    key = jax.random.key(seed)
    ks = jax.random.split(key, 32)
    f32 = jnp.float32

    def nrm(k, shape, scale=1.0):
        return jax.random.normal(k, shape, f32) * scale

    u = jax.random.uniform(ks[18], (DEPTH, LRU_WIDTH), f32, 0.9, 0.999)
    a_base = u ** (1.0 / LRU_C)
    lru_a_logit = jnp.log(a_base) - jnp.log1p(-a_base)
    return {
        "x_prompt": nrm(ks[0], (BATCH, SEQ, D_MODEL)),
        "x_sample": nrm(ks[1], (DEC_BATCH, DEC_SEQ, D_MODEL)),
        "mem_prompt": nrm(ks[2], (BATCH, MEM_TOKENS, D_MODEL)),
        "cache_sb_k": nrm(ks[3], (DEPTH, DEC_BATCH, SB_HEADS, PAST_LEN, SB_HEAD_DIM)),
        "cache_sb_v": nrm(ks[4], (DEPTH, DEC_BATCH, SB_HEADS, PAST_LEN, SB_HEAD_DIM)),
        "state_conv": nrm(ks[5], (DEPTH, DEC_BATCH, CONV_WIDTH - 1, LRU_WIDTH)),
        "state_lru_h": nrm(ks[6], (DEPTH, DEC_BATCH, LRU_WIDTH), 0.5),
        "cache_mem_k": nrm(ks[7], (DEPTH, DEC_BATCH, MEM_HEADS, MEM_TOKENS, MEM_HEAD_DIM)),
        "cache_mem_v": nrm(ks[8], (DEPTH, DEC_BATCH, MEM_HEADS, MEM_TOKENS, MEM_HEAD_DIM)),
        "norm_mix_g": 1.0 + nrm(ks[9], (DEPTH, D_MODEL), 0.01),
        "w_in": nrm(ks[10], (DEPTH, D_MODEL, IN_WIDTH), D_MODEL ** -0.5),
        "b_merge": nrm(ks[11], (DEPTH, N_BRANCH * D_MODEL), 0.01),
        "conv_w": nrm(ks[12], (DEPTH, CONV_WIDTH, LRU_WIDTH), CONV_WIDTH ** -0.5),
        "conv_b": nrm(ks[13], (DEPTH, LRU_WIDTH), 0.01),
        "lru_wa": nrm(ks[14], (DEPTH, LRU_BLOCKS, LRU_BLOCK_DIM, LRU_BLOCK_DIM), LRU_BLOCK_DIM ** -0.5),
        "lru_ba": nrm(ks[15], (DEPTH, LRU_WIDTH), 0.01),
        "lru_wx": nrm(ks[16], (DEPTH, LRU_BLOCKS, LRU_BLOCK_DIM, LRU_BLOCK_DIM), LRU_BLOCK_DIM ** -0.5),
        "lru_bx": nrm(ks[17], (DEPTH, LRU_WIDTH), 0.01),
        "lru_a_logit": lru_a_logit,
        "q_norm_g": 1.0 + nrm(ks[19], (DEPTH, MEM_HEAD_DIM), 0.01),
        "k_norm_g": 1.0 + nrm(ks[20], (DEPTH, MEM_HEAD_DIM), 0.01),
        "mem_norm_g": 1.0 + nrm(ks[21], (DEPTH, D_MODEL), 0.01),
        "w_mem_k": nrm(ks[22], (DEPTH, D_MODEL, MEM_WIDTH), D_MODEL ** -0.5),
        "w_mem_v": nrm(ks[23], (DEPTH, D_MODEL, MEM_WIDTH), D_MODEL ** -0.5),
        "w_br_sb": nrm(ks[24], (DEPTH, SB_WIDTH, D_MODEL), SB_WIDTH ** -0.5),
        "w_br_lru": nrm(ks[25], (DEPTH, LRU_WIDTH, D_MODEL), LRU_WIDTH ** -0.5),
        "w_br_mem": nrm(ks[26], (DEPTH, MEM_WIDTH, D_MODEL), MEM_WIDTH ** -0.5),
        "w_out": nrm(ks[27], (DEPTH, D_MODEL, D_MODEL), D_MODEL ** -0.5),
        "norm_ffn_g": 1.0 + nrm(ks[28], (DEPTH, D_MODEL), 0.01),
        "w_ffn_gate": nrm(ks[29], (DEPTH, D_MODEL, FFN_HIDDEN), D_MODEL ** -0.5),
        "w_ffn_up": nrm(ks[30], (DEPTH, D_MODEL, FFN_HIDDEN), D_MODEL ** -0.5),
        "w_ffn_down": nrm(ks[31], (DEPTH, FFN_HIDDEN, D_MODEL), FFN_HIDDEN ** -0.5),
    }


def reference(x_prompt, x_sample, mem_prompt, cache_sb_k, cache_sb_v, state_conv, state_lru_h,
              cache_mem_k, cache_mem_v, norm_mix_g, w_in, b_merge, conv_w, conv_b, lru_wa, lru_ba,
              lru_wx, lru_bx, lru_a_logit, q_norm_g, k_norm_g, mem_norm_g, w_mem_k, w_mem_v,
              w_br_sb, w_br_lru, w_br_mem, w_out, norm_ffn_g, w_ffn_gate, w_ffn_up, w_ffn_down):
    hp, hs = x_prompt, x_sample
    bp = x_prompt.shape[0]
    kp_l, vp_l, cp_l, sp_l, mkp_l, mvp_l = [], [], [], [], [], []
    ks_l, vs_l, cs_l, ss_l = [], [], [], []
    for l in range(DEPTH):
        lw = {
            "norm_mix_g": norm_mix_g[l], "w_in": w_in[l], "b_merge": b_merge[l],
            "conv_w": conv_w[l], "conv_b": conv_b[l], "lru_wa": lru_wa[l], "lru_ba": lru_ba[l],
            "lru_wx": lru_wx[l], "lru_bx": lru_bx[l], "lru_a_logit": lru_a_logit[l],
            "q_norm_g": q_norm_g[l], "w_br_sb": w_br_sb[l], "w_br_lru": w_br_lru[l],
            "w_br_mem": w_br_mem[l], "w_out": w_out[l], "norm_ffn_g": norm_ffn_g[l],
            "w_ffn_gate": w_ffn_gate[l], "w_ffn_up": w_ffn_up[l], "w_ffn_down": w_ffn_down[l],
        }
        mk_p, mv_p = memory_kv(mem_prompt, mem_norm_g[l], w_mem_k[l], w_mem_v[l], k_norm_g[l])
        empty = jnp.zeros((bp, SB_HEADS, 0, SB_HEAD_DIM), hp.dtype)
        hp, k_p, v_p, c_p, s_p = layer(hp, empty, empty,
                                       jnp.zeros((bp, CONV_WIDTH - 1, LRU_WIDTH), hp.dtype),
                                       jnp.zeros((bp, LRU_WIDTH), hp.dtype), mk_p, mv_p, lw)
        kp_l.append(k_p); vp_l.append(v_p); cp_l.append(c_p); sp_l.append(s_p)
        mkp_l.append(mk_p); mvp_l.append(mv_p)
        hs, k_s, v_s, c_s, s_s = layer(hs, cache_sb_k[l], cache_sb_v[l], state_conv[l], state_lru_h[l],
                                       cache_mem_k[l], cache_mem_v[l], lw)
        ks_l.append(k_s); vs_l.append(v_s); cs_l.append(c_s); ss_l.append(s_s)
    return (hp, hs,
            jnp.stack(kp_l), jnp.stack(vp_l), jnp.stack(cp_l), jnp.stack(sp_l),
            jnp.stack(mkp_l), jnp.stack(mvp_l),
            jnp.stack(ks_l), jnp.stack(vs_l), jnp.stack(cs_l), jnp.stack(ss_l))
```

```python
import numpy as np
import concourse.bass as bass
import concourse.mybir as mybir
from concourse.bass_utils import run_bass_kernel_spmd

F32 = mybir.dt.float32
BF16 = mybir.dt.bfloat16
AF = mybir.ActivationFunctionType
ALU = mybir.AluOpType

D = 1024
H = 8
DH = 128
MH = 4
MD = 256
MT = 256
FF = 2816
INW = 9216
EPS = 1e-6
NCORES = 8

PV_BM = 0
PV_CW = 24
PV_CB = 56
PV_BA = 64
PV_BX = 72
PV_AL = 80
PV_GQ = 88
NPV = 90
C_ID = 0
C_NTRI = 128
C_NONE = 256
C_ONE = 384
C_MASK = 512
NCST = 512 + 4 * 512


class StopBuild(Exception):
    pass


class Tracker:
    ROT = 30000

    def __init__(self, nc):
        self.nc = nc
        self.eng = {"pe": nc.tensor, "act": nc.scalar, "dve": nc.vector,
                    "pool": nc.gpsimd, "sp": nc.sync}
        self.sems = {}
        self.epoch = {e: 0 for e in ("pe", "act", "dve", "pool")}
        self.cnt = {e: 0 for e in ("pe", "act", "dve", "pool")}
        for e in self.cnt:
            self.sems[(e, 0)] = nc.alloc_semaphore(f"s_{e}_0")
        self.seen = {e: {} for e in self.eng}
        self.buf = {}
        self.dcnt = {}
        self.alias = {}
        self.nwait = 0
        self.pending = {}
        self.final = {}

    def _st(self, name):
        s = self.buf.get(name)
        if s is None:
            s = {"w": None, "r": {}}
            self.buf[name] = s
        return s

    def _names(self, name):
        return [name] + self.alias.get(name, [])

    def _deps(self, reads, writes):
        deps = {}

        def add(t):
            if t is None:
                return
            k, v = t
            if deps.get(k, 0) < v:
                deps[k] = v

        for r in reads:
            for n in self._names(r):
                add(self._st(n)["w"])
                if n[:2] in ("ps", "pt"):
                    for k, v in self._st(n)["r"].items():
                        add((k, v))
        for w in writes:
            for n in self._names(w):
                s = self._st(n)
                add(s["w"])
                for k, v in s["r"].items():
                    add((k, v))
        return deps

    def _emit_waits(self, e, deps):
        seen = self.seen[e]
        for k, v in deps.items():
            if e == "pe" and k[0] == "pe":
                continue
            if seen.get(k, 0) >= v:
                continue
            if k[0] in self.cnt and k[1] == self.epoch[k[0]]:
                assert self.cnt[k[0]] >= v, f"wait on unsignalled ticket {k} {v}"
            self.eng[e].wait_ge(self.sems[k], v)
            seen[k] = v
            self.nwait += 1

    def _record(self, reads, writes, t):
        for r in reads:
            s = self._st(r)
            if s["r"].get(t[0], 0) < t[1]:
                s["r"][t[0]] = t[1]
        for w in writes:
            s = self._st(w)
            s["w"] = t
            s["r"] = {}

    def op(self, e, reads, writes, fn, signal=True):
        if self.cnt[e] >= self.ROT and not self.pending.get(e, False):
            self.final[(e, self.epoch[e])] = self.cnt[e]
            self.epoch[e] += 1
            self.cnt[e] = 0
            self.sems[(e, self.epoch[e])] = self.nc.alloc_semaphore(f"s_{e}_{self.epoch[e]}")
        self._emit_waits(e, self._deps(reads, writes))
        ins = fn(self.eng[e])
        if signal:
            self.cnt[e] += 1
            key = (e, self.epoch[e])
            ins.then_inc(self.sems[key], 1)
            t = (key, self.cnt[e])
            self.pending[e] = False
        else:
            t = ((e, self.epoch[e]), self.cnt[e] + 1)
            self.pending[e] = True
        self._record(reads, writes, t)
        return ins

    def dma(self, q, reads, writes, fn, slot):
        self._emit_waits(q, self._deps(reads, writes))
        key = ("dma", slot + "_" + q)
        if key not in self.sems:
            self.sems[key] = self.nc.alloc_semaphore(f"d_{slot}_{q}")
            self.dcnt[key] = 0
        ins = fn(self.eng[q])
        self.dcnt[key] += 16
        ins.then_inc(self.sems[key], 16)
        self._record(reads, writes, (key, self.dcnt[key]))
        return ins

    def final_wait(self, e):
        deps = {}
        for k in self.sems:
            if k[0] == "dma":
                deps[k] = self.dcnt[k]
            else:
                deps[k] = self.cnt[k[0]] if k[1] == self.epoch[k[0]] else self.final[k]
        self._emit_waits(e, deps)


def build(T, PAST, SBN=4, SQ=64):
    import os
    KD = os.environ.get("KDBG", "")
    STQ = os.environ.get("KSTQ", "sp")
    nc = bass.Bass("TRN2", target_bir_lowering=False)
    tr = Tracker(nc)
    NTILE = T // 512
    NPB = PAST // 128

    def din(name, shape, dt=F32):
        return nc.dram_tensor(name, list(shape), dt, kind="ExternalInput").ap()

    def dout(name, shape, dt=F32):
        return nc.dram_tensor(name, list(shape), dt, kind="ExternalOutput").ap()

    xp = din("xp", [T, D]); xs = din("xs", [SBN * SQ, D]); memp = din("memp", [MT, D])
    csk = din("csk", [SBN, H, PAST, DH]); csv = din("csv", [SBN, H, PAST, DH])
    sconv = din("sconv", [128, 8, SBN, 3]); slru = din("slru", [128, 8, SBN])
    cmk = din("cmk", [SBN, MH, MT, MD]); cmv = din("cmv", [SBN, MH, MT, MD])
    pvec_d = din("pvec", [128, NPV]); bvec_d = din("bvec", [1, 3 * D + MD]); cst_d = din("cst", [128, NCST])
    W = {
        "w_in": din("w_in", [D, INW]), "w_mem_k": din("w_mem_k", [D, D]), "w_mem_v": din("w_mem_v", [D, D]),
        "w_br_sb": din("w_br_sb", [D, D]), "w_br_lru": din("w_br_lru", [D, D]), "w_br_mem": din("w_br_mem", [D, D]),
        "w_out": din("w_out", [D, D]), "w_ffn_gate": din("w_ffn_gate", [D, FF]), "w_ffn_up": din("w_ffn_up", [D, FF]),
        "w_ffn_down": din("w_ffn_down", [FF, D]),
    }
    lru_wa_d = din("lru_wa", [8, 128, 128]); lru_wx_d = din("lru_wx", [8, 128, 128])

    yp = dout("yp", [T, D]); ys = dout("ys", [SBN * SQ, D])
    kp = dout("kp", [H, DH, T]); vp = dout("vp", [H, T, DH])
    cp = dout("cp", [128, 8, 3]); hp = dout("hp", [128, 8])
    mkp = dout("mkp", [MH, MT, MD]); mvp = dout("mvp", [MH, MT, MD])
    ksn = dout("ksn", [H, DH, SBN * SQ]); vsn = dout("vsn", [SBN, H, SQ, DH])
    cs = dout("cs", [128, 8, SBN, 3]); hs = dout("hs", [128, 8, SBN])

    SL = {}

    def addslab(key, w, r0, kc, c0, ncols):
        SL[key] = (len(SL), w, r0, kc, c0, ncols)

    for j in range(18):
        addslab(("in", j), "w_in", 0, 8, j * 512, 512)
    for b, nm in enumerate(["w_br_sb", "w_br_lru", "w_br_mem"]):
        for j in range(2):
            addslab(("br", b, j), nm, 0, 8, j * 512, 512)
    for j in range(2):
        addslab(("out", j), "w_out", 0, 8, j * 512, 512)
    for j in range(6):
        addslab(("fg", j), "w_ffn_gate", 0, 8, j * 512, min(512, FF - j * 512))
        addslab(("fu", j), "w_ffn_up", 0, 8, j * 512, min(512, FF - j * 512))
    for half in range(2):
        for kg in range(3):
            addslab(("fd", half, kg), "w_ffn_down", kg * 1024, min(8, 22 - kg * 8), half * 512, 512)
    for j in range(2):
        addslab(("mk", j), "w_mem_k", 0, 8, j * 512, 512)
        addslab(("mv", j), "w_mem_v", 0, 8, j * 512, 512)
    wscr = nc.dram_tensor("wscr", [len(SL), 128, 8, 512], BF16, kind="Internal").ap()
    ktscr = nc.dram_tensor("ktscr", [H, DH, T], BF16, kind="Internal").ap()
    vscr = nc.dram_tensor("vscr", [H, 128, T // 128, DH], BF16, kind="Internal").ap()

    sb = {}

    def salloc(name, shape, dt):
        sb[name] = nc.alloc_sbuf_tensor("sb_" + name, list(shape), dt)
        return sb[name]

    salloc("cst", [128, NCST], BF16)
    salloc("pvec", [128, NPV], F32)
    salloc("bvec", [128, 3 * D + MD], F32)
    salloc("lrc", [128, 4, 8], F32)
    salloc("gq16", [128, 2], F32)
    salloc("wa", [128, 8, 128], BF16); salloc("wx", [128, 8, 128], BF16)
    salloc("hist", [128, 8, 3], F32); salloc("hcar", [128, 8], F32)
    salloc("csam", [128, 8, SBN, 3], F32); salloc("hsam", [128, 8, SBN], F32)
    salloc("sconv", [128, 8, SBN, 3], F32); salloc("slru", [128, 8, SBN], F32)
    salloc("xt", [128, 4, D], F32)
    salloc("xnb", [128, 4, D], BF16)
    salloc("Vbf", [128, 4, D], BF16)
    salloc("xnT", [128, 8, 512], BF16)
    big = salloc("big", [128, 24, 512], BF16)
    salloc("olruT", [128, 8, 512], BF16)
    salloc("osbT", [128, 8, 512], BF16)
    sb["omemT"] = sb["Vbf"][:, :, :].rearrange("p c (a f) -> p (c a) f", f=512)
    sb["mergedT"] = sb["xnb"][:, :, :].rearrange("p c (a f) -> p (c a) f", f=512)
    NWR = 3
    for i in range(NWR):
        salloc(f"wr{i}", [128, 8, 512], BF16)
    NKR = 3
    for i in range(NKR):
        salloc(f"kr{i}", [128, 1024], BF16); salloc(f"vr{i}", [128, 8, 128], BF16)
    salloc("memKT", [128, MH, 2, MT], BF16); salloc("memV", [128, 2, D], BF16)
    salloc("stat", [128, 16], F32); salloc("mstat", [128, 4], F32)
    for nm_ in ("qf0", "qf1", "rsq"):
        salloc(nm_, [128, 512], F32)
    for nm_ in ("sq0", "sq1", "qn0", "qn1"):
        salloc(nm_, [128, 512], BF16)
    NTF = 8
    for i in range(NTF):
        salloc(f"tf{i}", [128, 520], F32)
    NTB = 10
    for i in range(NTB):
        salloc(f"tb{i}", [128, 512], BF16)
    for i in range(2):
        salloc(f"R32_{i}", [128, 512], F32); salloc(f"R16_{i}", [128, 512], BF16)
    tr.alias = {"hT": ["QT", "KTn", "gel"], "QT": ["hT"], "KTn": ["hT"], "gel": ["hT"],
                "omemT": ["Vbf"], "Vbf": ["omemT"], "mergedT": ["xnb"], "xnb": ["mergedT"]}
    QT = big[:, 0:8, :]; KTn = big[:, 8:16, :]; gel = big[:, 16:24, :]; hT = big[:, 0:22, :]

    ps = [nc.alloc_psum_tensor(f"ps{i}", [128, 512], F32) for i in range(6)]
    pt = [nc.alloc_psum_tensor(f"pt{i}", [128, 1024], BF16) for i in range(2)]
    rr = {"ps": 0, "pt": 0, "tf": 0, "tb": 0, "wr": 0, "kr": 0}

    def nxt(kind, n):
        i = rr[kind] % n
        rr[kind] = (i + 1) % n
        return i

    def nps(n=6):
        i = nxt("ps", n)
        return f"ps{i}", ps[i]

    def npt():
        i = nxt("pt", 2)
        return f"pt{i}", pt[i]

    def ntf():
        i = nxt("tf", NTF)
        return f"tf{i}", sb[f"tf{i}"]

    def ntb():
        i = nxt("tb", NTB)
        return f"tb{i}", sb[f"tb{i}"]

    cst = sb["cst"]
    ident = cst[:, C_ID:C_ID + 128]
    ntri = cst[:, C_NTRI:C_NTRI + 128]
    none_ = cst[:, C_NONE:C_NONE + 128]
    ones = cst[:, C_ONE:C_ONE + 128]
    pvec = sb["pvec"]; bvec = sb["bvec"]
    G_MIX = bvec[:, 0:D]; G_FFN = bvec[:, D:2 * D]; G_MEM = bvec[:, 2 * D:3 * D]; G_K = bvec[:, 3 * D:3 * D + MD]

    tr.dma("sp", [], ["pvec"], lambda e: e.dma_start(out=pvec[:], in_=pvec_d), "pvec")
    tr.dma("sp", [], ["bvec"], lambda e: e.dma_start(out=bvec[:], in_=bvec_d[0:1, :].partition_broadcast(128)), "bvec")
    tr.dma("sp", [], ["sconv"], lambda e: e.dma_start(out=sb["sconv"][:], in_=sconv), "sconv")
    tr.dma("sp", [], ["slru"], lambda e: e.dma_start(out=sb["slru"][:], in_=slru), "slru")
    for wn_, wd_ in (("wa", lru_wa_d), ("wx", lru_wx_d)):
        for n0 in (0, 4):
            fn_, f_ = ntf()
            tr.dma("sp", [], [fn_], lambda e: e.dma_start(out=f_[:, 0:512].rearrange("p (n e) -> p n e", n=4), in_=wd_[n0:n0 + 4].rearrange("n d e -> d n e")), fn_)
            tr.op("dve", [fn_], [wn_], lambda e: e.tensor_copy(out=sb[wn_][:, n0:n0 + 4, :], in_=f_[:, 0:512].rearrange("p (n e) -> p n e", n=4)))
    for ci in range(NCST // 512):
        fn_, f_ = ntf()
        tr.dma("sp", [], [fn_], lambda e: e.dma_start(out=f_[:, 0:512], in_=cst_d[:, ci * 512:(ci + 1) * 512]), fn_)
        tr.op("dve", [fn_], ["cst"], lambda e: e.tensor_copy(out=cst[:, ci * 512:(ci + 1) * 512], in_=f_[:, 0:512]))
    tr.op("dve", [], ["hist"], lambda e: e.memset(sb["hist"][:], 0.0))
    tr.op("dve", [], ["hcar"], lambda e: e.memset(sb["hcar"][:], 0.0))
    lrc = sb["lrc"]
    tr.op("act", ["pvec"], ["lrc2"], lambda e: e.activation(out=lrc[:, 2, :], in_=pvec[:, PV_AL:PV_AL + 8], func=AF.Exp, scale=-1.0))
    tr.op("act", ["lrc2"], ["lrc3"], lambda e: e.activation(out=lrc[:, 3, :], in_=lrc[:, 2, :], func=AF.Ln, bias=1.0))
    tr.op("dve", ["lrc3"], ["lrc0"], lambda e: e.tensor_scalar(out=lrc[:, 0, :], in0=lrc[:, 3, :], scalar1=-8.0, scalar2=None, op0=ALU.mult))
    tr.op("dve", ["lrc3"], ["lrc1"], lambda e: e.tensor_scalar(out=lrc[:, 1, :], in0=lrc[:, 3, :], scalar1=-16.0, scalar2=None, op0=ALU.mult))
    tr.op("dve", ["pvec"], ["gq16"], lambda e: e.tensor_scalar(out=sb["gq16"][:], in0=pvec[:, PV_GQ:PV_GQ + 2], scalar1=1.0 / 16.0, scalar2=None, op0=ALU.mult))

    touched = set()

    sched = {"list": [], "pos": 0}
    pending = {}

    def load_slab(key, scratch=True):
        if key in pending:
            res = pending.pop(key)
        else:
            res = issue_load(key, scratch)
        L = sched["list"]
        try:
            i = L.index(key, sched["pos"])
        except ValueError:
            i = None
        if i is not None:
            sched["pos"] = i + 1
            if i + 1 < len(L) and L[i + 1] not in pending:
                pending[L[i + 1]] = issue_load(L[i + 1], True)
        return res

    def issue_load(key, scratch=True):
        idx, wname, r0, kc, c0, ncols = SL[key]
        i = nxt("wr", NWR)
        slot = f"wr{i}"
        t = sb[slot]
        if key not in touched:
            touched.add(key)
            for k in range(kc):
                fn_, f_ = ntf()
                tr.dma("sp", [], [fn_], lambda e: e.dma_start(out=f_[:, 0:ncols], in_=W[wname][r0 + k * 128:r0 + (k + 1) * 128, c0:c0 + ncols]), fn_)
                tr.op("dve", [fn_], [slot], lambda e: e.tensor_copy(out=t[:, k, 0:ncols], in_=f_[:, 0:ncols]))
            if scratch:
                tr.dma(STQ, [slot], [f"wscr{idx}"], lambda e: e.dma_start(out=wscr[idx, :, 0:kc, 0:ncols], in_=t[:, 0:kc, 0:ncols]), slot)
        else:
            tr.dma("sp", [f"wscr{idx}"], [slot], lambda e: e.dma_start(out=t[:, 0:kc, 0:ncols], in_=wscr[idx, :, 0:kc, 0:ncols]), slot)
        return slot, t

    def rmsnorm_T(src_name, src, g_ap, CR, nch, dstT_name, dstT, ncolsT):
        stat = sb["stat"]
        tr.op("dve", [], ["stat_a"], lambda e: e.memset(stat[:, 0:4], 0.0))
        for c in range(nch):
            tr.op("act", [src_name, "stat_a"], ["xnb", "stat_a"], lambda e: e.activation(out=sb["xnb"][0:CR, c, :], in_=src[0:CR, c, :], func=AF.Square, accum_out=stat[0:CR, c:c + 1]))
        tr.op("dve", ["stat_a"], ["stat_b"], lambda e: e.tensor_scalar(out=stat[0:CR, 4:4 + nch], in0=stat[0:CR, 0:nch], scalar1=1.0 / D, scalar2=EPS, op0=ALU.mult, op1=ALU.add))
        tr.op("act", ["stat_b"], ["stat_c"], lambda e: e.activation(out=stat[0:CR, 8:8 + nch], in_=stat[0:CR, 4:4 + nch], func=AF.Ln))
        tr.op("act", ["stat_c"], ["stat_d"], lambda e: e.activation(out=stat[0:CR, 12:12 + nch], in_=stat[0:CR, 8:8 + nch], func=AF.Exp, scale=-0.5))
        for c in range(nch):
            tr.op("dve", [src_name, "stat_d", "bvec"], ["xnb"], lambda e: e.scalar_tensor_tensor(
                out=sb["xnb"][0:CR, c, :], in0=src[0:CR, c, :], scalar=stat[0:CR, 12 + c:13 + c], in1=g_ap[0:CR, :], op0=ALU.mult, op1=ALU.mult))
        for k in range(8):
            pn, p = npt()
            for c in range(nch):
                tr.op("pe", ["xnb", "cst"], [pn], lambda e: e.transpose(out=p[:, c * CR:(c + 1) * CR], in_=sb["xnb"][0:CR, c, k * 128:(k + 1) * 128], identity=ident[0:CR, 0:CR]), signal=(c == nch - 1))
            eng = "act" if k % 2 == 0 else "dve"
            if eng == "act":
                tr.op("act", [pn], [dstT_name], lambda e: e.copy(out=dstT[:, k, 0:ncolsT], in_=p[:, 0:ncolsT]))
            else:
                tr.op("dve", [pn], [dstT_name], lambda e: e.tensor_copy(out=dstT[:, k, 0:ncolsT], in_=p[:, 0:ncolsT]))

    def mm_fm(out_ps, pn, slab_name, slab, col, rhs_name, rhsT, nt, kc=8, extra_reads=()):
        for k in range(kc):
            tr.op("pe", [slab_name, rhs_name] + list(extra_reads), [pn], lambda e: e.matmul(out_ps[:, 0:nt], lhsT=slab[:, k, col:col + 128], rhs=rhsT[:, k, 0:nt], start=(k == 0), stop=(k == kc - 1)), signal=(k == kc - 1))

    class Piece:
        def __init__(self, fn):
            self.fn = fn
            self.done = False

        def ensure(self):
            if not self.done:
                self.done = True
                self.fn(self)

    def run_attention(blocks):
        n = len(blocks)
        LOOK = 6
        st = {}
        grp = {"i": 0}

        def S1(b):
            zn, z = nps(4)
            b["zn"], b["z"] = zn, z
            nk, nq = b["nk"], b["nq"]
            kT = b["kT"](); b["vap"] = b["v"]()
            tr.op("pe", b["reads"], [zn], lambda e: e.matmul(z[0:nk, 0:nq], lhsT=kT, rhs=b["q"], start=True, stop=True))
            en, E = ntf()
            tr.op("act", [zn], [en], lambda e: e.activation(out=E[0:nk, 0:nq], in_=z[0:nk, 0:nq], func=AF.Exp))
            sn, SP = ntb()
            b["sn"], b["SP"] = sn, SP
            tr.op("act", [en], [sn], lambda e: e.activation(out=SP[0:nk, 0:nq], in_=E[0:nk, 0:nq], func=AF.Ln, bias=1.0))
            if b["mask"] is not None:
                m = b["mask"]
                tr.op("dve", [sn, "cst"], [sn], lambda e: e.tensor_tensor(out=SP[0:nk, 0:nq], in0=SP[0:nk, 0:nq], in1=m, op=ALU.mult))

        def S2(b):
            nk, nq = b["nk"], b["nq"]
            c0 = b.get("c0", 0)
            z, zn, SP, sn = b["z"], b["zn"], b["SP"], b["sn"]
            if b["first"]:
                grp["i"] ^= 1
                g = grp["i"]
                tr.op("dve", [], [f"R32_{g}"], lambda e: e.memset(sb[f"R32_{g}"][:, 0:c0 + nq], 0.0))
            g = grp["i"]
            R32n, R16n = f"R32_{g}", f"R16_{g}"
            R32, R16 = sb[R32n], sb[R16n]
            tr.op("pe", [sn, "cst"], [zn], lambda e: e.matmul(z[0:nk, 0:nq], lhsT=ntri[0:nk, 0:nk], rhs=SP[0:nk, 0:nq], start=False, stop=b["first"], skip_group_check=True), signal=b["first"])
            if not b["first"]:
                tr.op("pe", [R16n, "cst"], [zn], lambda e: e.matmul(z[0:nk, 0:nq], lhsT=none_[:, 0:nk], rhs=R16[:, c0:c0 + nq], start=False, stop=True, skip_group_check=True))
            an, A = ntb()
            b["an"], b["A"] = an, A
            tr.op("act", [zn], [an], lambda e: e.activation(out=A[0:nk, 0:nq], in_=z[0:nk, 0:nq], func=AF.Exp))
            if b["mask"] is not None:
                m = b["mask"]
                tr.op("dve", [an, "cst"], [an], lambda e: e.tensor_tensor(out=A[0:nk, 0:nq], in0=A[0:nk, 0:nq], in1=m, op=ALU.mult))
            if not b["last"]:
                tr.op("dve", [sn, R32n], [R32n], lambda e: e.tensor_tensor(out=R32[0:nk, c0:c0 + nq], in0=R32[0:nk, c0:c0 + nq], in1=SP[0:nk, 0:nq], op=ALU.add))
                tr.op("dve", [R32n], [R16n], lambda e: e.tensor_copy(out=R16[:, 0:c0 + nq], in_=R32[:, 0:c0 + nq]))

        def S3(b):
            nk, nq = b["nk"], b["nq"]
            if b["first"]:
                st["on"], st["o"] = f"ps{4 + (st.get('oi', 0))}", ps[4 + st.get("oi", 0)]
                st["oi"] = 1 - st.get("oi", 0)
            on, o = st["on"], st["o"]
            vap, A, an = b["vap"], b["A"], b["an"]
            c0 = b.get("c0", 0)
            tr.op("pe", [an] + b["reads"], [on], lambda e: e.matmul(o[:, c0:c0 + nq], lhsT=vap, rhs=A[0:nk, 0:nq], start=b["first"], stop=b["last"], skip_group_check=True), signal=b["last"])
            if b["last"]:
                b["out"](on, o)

        for i in range(n + 2):
            if i < n:
                for j in range(i, min(n, i + LOOK)):
                    if blocks[j]["piece"] is not None:
                        blocks[j]["piece"].ensure()
                S1(blocks[i])
            if 0 <= i - 1 < n:
                S2(blocks[i - 1])
            if 0 <= i - 2 < n:
                S3(blocks[i - 2])

    def mem_prompt_prep():
        mt = sb["xt"]
        tr.dma("sp", [], ["xt"], lambda e: e.dma_start(out=mt[:, 0:2, :], in_=memp.rearrange("(c p) f -> p c f", p=128)), "xt")
        rmsnorm_T("xt", mt, G_MEM, 128, 2, "xnT", sb["xnT"], 256)
        if KD == "mem1":
            return
        mT = sb["xnT"]
        kf = sb["Vbf"]
        for which in ("mk", "mv"):
            if KD == "mem8" and which == "mv":
                return
            for half in range(2):
                sn_, slab = load_slab((which, half), scratch=False)
                if KD == "mem2":
                    return
                for c in range(2):
                    pn, p = nps()
                    for k in range(8):
                        tr.op("pe", [sn_, "xnT"], [pn], lambda e: e.matmul(p[:, :], lhsT=mT[:, k, c * 128:(c + 1) * 128], rhs=slab[:, k, :], start=(k == 0), stop=(k == 7)), signal=(k == 7))
                    if KD == "mem3":
                        return
                    for hh in range(2):
                        hm = half * 2 + hh
                        fn, f = ntf()
                        if which == "mv":
                            tr.op("act", [pn], [fn], lambda e: e.copy(out=f[:, 0:256], in_=p[:, hh * 256:(hh + 1) * 256]))
                            tr.op("dve", [pn], ["memV"], lambda e: e.tensor_copy(out=sb["memV"][:, c, hm * 256:(hm + 1) * 256], in_=p[:, hh * 256:(hh + 1) * 256]))
                            tr.dma(STQ, [fn], ["mvp"], lambda e: e.dma_start(out=mvp[hm, c * 128:(c + 1) * 128, :], in_=f[:, 0:256]), fn)
                        else:
                            stat = sb["mstat"]
                            jn, jk = ntf()
                            tr.op("dve", [], ["ms_a"], lambda e: e.memset(stat[:, 0:1], 0.0))
                            tr.op("act", [pn, "ms_a"], [jn, "ms_a"], lambda e: e.activation(out=jk[:, 0:256], in_=p[:, hh * 256:(hh + 1) * 256], func=AF.Square, accum_out=stat[:, 0:1]))
                            if KD == "mem4":
                                return
                            tr.op("dve", ["ms_a"], ["ms_b"], lambda e: e.tensor_scalar(out=stat[:, 1:2], in0=stat[:, 0:1], scalar1=1.0 / MD, scalar2=EPS, op0=ALU.mult, op1=ALU.add))
                            tr.op("act", ["ms_b"], ["ms_c"], lambda e: e.activation(out=stat[:, 2:3], in_=stat[:, 1:2], func=AF.Ln))
                            tr.op("act", ["ms_c"], ["ms_d"], lambda e: e.activation(out=stat[:, 3:4], in_=stat[:, 2:3], func=AF.Exp, scale=-0.5))
                            tr.op("dve", [pn, "ms_d", "bvec", jn], [fn], lambda e: e.scalar_tensor_tensor(out=f[:, 0:256], in0=p[:, hh * 256:(hh + 1) * 256], scalar=stat[:, 3:4], in1=G_K, op0=ALU.mult, op1=ALU.mult))
                            if KD == "mem5":
                                return
                            tr.dma(STQ, [fn], ["mkp"], lambda e: e.dma_start(out=mkp[hm, c * 128:(c + 1) * 128, :], in_=f[:, 0:256]), fn)
                            if KD == "mem6":
                                return
                            tr.op("dve", [fn], ["Vbf"], lambda e: e.tensor_copy(out=kf[:, c, hm * 256:(hm + 1) * 256], in_=f[:, 0:256]))
                            if KD == "mem7":
                                return
        if KD == "mem9":
            return
        memKT_from_tm(kf, "Vbf")

    def memKT_from_tm(ktm, ktm_name):
        for hm in range(MH):
            for dc in range(2):
                pn, p = npt()
                for c in range(2):
                    tr.op("pe", [ktm_name, "cst"], [pn], lambda e: e.transpose(out=p[:, c * 128:(c + 1) * 128], in_=ktm[:, c, hm * 256 + dc * 128: hm * 256 + (dc + 1) * 128], identity=ident), signal=(c == 1))
                tr.op("dve", [pn, "gq16"], ["memKT"], lambda e: e.tensor_scalar(out=sb["memKT"][:, hm, dc, :], in0=p[:, 0:256], scalar1=sb["gq16"][:, dc:dc + 1], scalar2=None, op0=ALU.mult))

    def tile(kind, ti):
        prompt = kind == "p"
        CR = 128 if prompt else SQ
        nt = 4 * CR
        NSEG, SLEN = (1, 512) if prompt else (SBN, SQ)
        t0 = ti * 512
        xt = sb["xt"]; xnT = sb["xnT"]; Vbf = sb["Vbf"]
        if prompt:
            tr.dma("sp", [], ["xt"], lambda e: e.dma_start(out=xt[:, :, :], in_=xp[t0:t0 + 512, :].rearrange("(c p) f -> p c f", p=128)), "xt")
        else:
            tr.dma("sp", [], ["xt"], lambda e: e.dma_start(out=xt[0:CR, :, :], in_=xs.rearrange("(c p) f -> p c f", p=CR)), "xt")
        rmsnorm_T("xt", xt, G_MIX, CR, 4, "xnT", xnT, nt)

        if KD == "t_norm":
            raise StopBuild()
        for which in range(2):
            for half in range(2):
                sn_, slab = load_slab(("in", which * 2 + half))
                for hh in range(4):
                    h = half * 4 + hh
                    pn, p = nps()
                    mm_fm(p, pn, sn_, slab, hh * 128, "xnT", xnT, nt)
                    if which == 0:
                        tr.op("act", [pn], ["QT"], lambda e: e.mul(out=QT[:, h, 0:nt], in_=p[:, 0:nt], mul=float(1.0 / np.sqrt(DH))))
                    else:
                        tr.op("act", [pn], ["KTn"], lambda e: e.copy(out=KTn[:, h, 0:nt], in_=p[:, 0:nt]))
                        fn, f = ntf()
                        tr.op("dve", [pn], [fn], lambda e: e.tensor_copy(out=f[:, 0:nt], in_=p[:, 0:nt]))
                        if prompt:
                            tr.dma(STQ, [fn], ["kp"], lambda e: e.dma_start(out=kp[h, :, t0:t0 + 512], in_=f[:, 0:512]), fn)
                        else:
                            tr.dma(STQ, [fn], ["ksn"], lambda e: e.dma_start(out=ksn[h, :, :], in_=f[:, 0:nt]), fn)
        if prompt and ti < NTILE - 1:
            tr.dma(STQ, ["KTn"], [f"kts{ti}"], lambda e: e.dma_start(out=ktscr[:, :, t0:t0 + 512].rearrange("h d t -> d h t"), in_=KTn[:, :, :]), "KTn")
        if KD == "t_qk":
            raise StopBuild()
        for half in range(2):
            sn_, slab = load_slab(("in", 4 + half))
            for c in range(4):
                pn, p = nps()
                for k in range(8):
                    tr.op("pe", [sn_, "xnT"], [pn], lambda e: e.matmul(p[0:CR, :], lhsT=xnT[:, k, c * CR:(c + 1) * CR], rhs=slab[:, k, :], start=(k == 0), stop=(k == 7)), signal=(k == 7))
                tr.op("act", [pn], ["Vbf"], lambda e: e.copy(out=Vbf[0:CR, c, half * 512:(half + 1) * 512], in_=p[0:CR, :]))
                fn, f = ntf()
                tr.op("dve", [pn], [fn], lambda e: e.tensor_copy(out=f[0:CR, 0:512], in_=p[0:CR, :]))
                if prompt:
                    tr.dma(STQ, [fn], ["vp"], lambda e: e.dma_start(out=vp[half * 4:half * 4 + 4, t0 + c * 128:t0 + (c + 1) * 128, :].rearrange("h t d -> t h d"), in_=f[:, 0:512].rearrange("p (h d) -> p h d", h=4)), fn)
                else:
                    tr.dma(STQ, [fn], ["vsn"], lambda e: e.dma_start(out=vsn[c, half * 4:half * 4 + 4, :, :].rearrange("h t d -> t h d"), in_=f[0:CR, 0:512].rearrange("p (h d) -> p h d", h=4)), fn)
        if prompt and ti < NTILE - 1:
            for c in range(4):
                tr.dma(STQ, ["Vbf"], [f"vs{ti}"], lambda e: e.dma_start(out=vscr[:, :, ti * 4 + c, :].rearrange("h p d -> p h d"), in_=Vbf[:, c, :].rearrange("p (h d) -> p h d", h=H)), "Vbf")

        if KD == "t_v":
            raise StopBuild()
        blocks = []
        if prompt:
            for h in range(H):
                q = QT[:, h, 0:512]
                first = True
                total = 4 * (ti + 1)
                cntb = 0

                def outfn(on, o, h=h):
                    tr.op("dve", [on], ["osbT"], lambda e: e.tensor_copy(out=sb["osbT"][:, h, 0:512], in_=o[:, 0:512]))

                for j in (3, 2, 1, 0):
                    cntb += 1
                    blocks.append(dict(q=QT[:, h, j * 128:512], nq=512 - j * 128, c0=j * 128, nk=128, kT=(lambda h=h, j=j: KTn[:, h, j * 128:(j + 1) * 128]),
                                       v=(lambda h=h, j=j: Vbf[:, j, h * 128:(h + 1) * 128]),
                                       mask=cst[:, C_MASK + j * 512 + j * 128:C_MASK + (j + 1) * 512], first=first, last=(cntb == total),
                                       reads=["QT", "KTn", "Vbf"], piece=None, out=outfn))
                    first = False
                tj = ti
                while tj > 0:
                    lo = max(0, tj - 2)
                    ntl = tj - lo
                    holder = {}

                    def issue(pc, h=h, lo=lo, ntl=ntl, holder=holder):
                        i = nxt("kr", NKR)
                        holder["i"] = i
                        kr, vr = sb[f"kr{i}"], sb[f"vr{i}"]
                        rd = [f"kts{t_}" for t_ in range(lo, lo + ntl)]
                        rv = [f"vs{t_}" for t_ in range(lo, lo + ntl)]
                        tr.dma("sp", rd, [f"kr{i}"], lambda e: e.dma_start(out=kr[:, 0:ntl * 512], in_=ktscr[h, :, lo * 512:(lo + ntl) * 512]), f"kr{i}")
                        tr.dma("sp", rv, [f"vr{i}"], lambda e: e.dma_start(out=vr[:, 0:ntl * 4, :], in_=vscr[h, :, lo * 4:(lo + ntl) * 4, :]), f"vr{i}")

                    pc = Piece(issue)
                    for jb in range(ntl * 4 - 1, -1, -1):
                        cntb += 1
                        blocks.append(dict(q=q, nq=512, nk=128,
                                           kT=(lambda holder=holder, jb=jb: sb[f"kr{holder['i']}"][:, jb * 128:(jb + 1) * 128]),
                                           v=(lambda holder=holder, jb=jb: sb[f"vr{holder['i']}"][:, jb, :]),
                                           mask=None, first=False, last=(cntb == total), reads=["QT"], piece=pc, out=outfn,
                                           holder=holder))
                    tj = lo
            for b in blocks:
                if b["piece"] is not None:
                    hd = b["holder"]
                    b["reads"] = _DynReads(hd)
        else:
            for s in range(SBN):
                for h in range(H):
                    q = QT[:, h, s * SQ:(s + 1) * SQ]

                    def outfn(on, o, h=h, s=s):
                        tr.op("dve", [on], ["osbT"], lambda e: e.tensor_copy(out=sb["osbT"][:, h, s * SQ:(s + 1) * SQ], in_=o[:, 0:SQ]))

                    blocks.append(dict(q=q, nq=SQ, nk=SQ, kT=(lambda h=h, s=s: KTn[:, h, s * SQ:(s + 1) * SQ]),
                                       v=(lambda h=h, s=s: Vbf[0:SQ, s, h * 128:(h + 1) * 128]),
                                       mask=cst[0:SQ, C_MASK:C_MASK + SQ], first=True, last=(NPB == 0),
                                       reads=["QT", "KTn", "Vbf"], piece=None, out=outfn))
                    holder = {}

                    def issue(pc, h=h, s=s, holder=holder):
                        i = nxt("kr", NKR)
                        holder["i"] = i
                        kr, vr = sb[f"kr{i}"], sb[f"vr{i}"]
                        tbn, tb_ = ntb()
                        tbn2, tb2 = ntb()
                        for half in range(2):
                            tgt = tb_ if half == 0 else tb2
                            tn_ = tbn if half == 0 else tbn2
                            nb = min(4, NPB - half * 4)
                            if nb <= 0:
                                continue
                            fn_, f_ = ntf()
                            tr.dma("sp", [], [fn_], lambda e: e.dma_start(out=f_[:, 0:nb * 128].rearrange("p (b d) -> p b d", d=128), in_=csk[s, h, half * 512:half * 512 + nb * 128, :].rearrange("(b p) d -> p b d", p=128)), fn_)
                            tr.op("dve", [fn_], [tn_], lambda e: e.tensor_copy(out=tgt[:, 0:nb * 128], in_=f_[:, 0:nb * 128]))
                            fn2_, f2_ = ntf()
                            tr.dma("sp", [], [fn2_], lambda e: e.dma_start(out=f2_[:, 0:nb * 128].rearrange("p (b d) -> p b d", d=128), in_=csv[s, h, half * 512:half * 512 + nb * 128, :].rearrange("(b p) d -> p b d", p=128)), fn2_)
                            tr.op("dve", [fn2_], [f"vr{i}"], lambda e: e.tensor_copy(out=vr[:, half * 4:half * 4 + nb, :], in_=f2_[:, 0:nb * 128].rearrange("p (b d) -> p b d", d=128)))
                        pn, p = npt()
                        for bI in range(NPB):
                            tgt = tb_ if bI < 4 else tb2
                            tn_ = tbn if bI < 4 else tbn2
                            tr.op("pe", [tn_, "cst"], [pn], lambda e: e.transpose(out=p[:, bI * 128:(bI + 1) * 128], in_=tgt[:, (bI % 4) * 128:(bI % 4 + 1) * 128], identity=ident), signal=(bI == NPB - 1))
                        tr.op("dve", [pn], [f"kr{i}"], lambda e: e.tensor_copy(out=kr[:, 0:NPB * 128], in_=p[:, 0:NPB * 128]))

                    pc = Piece(issue)
                    for jb in range(NPB - 1, -1, -1):
                        blocks.append(dict(q=q, nq=SQ, nk=128,
                                           kT=(lambda holder=holder, jb=jb: sb[f"kr{holder['i']}"][:, jb * 128:(jb + 1) * 128]),
                                           v=(lambda holder=holder, jb=jb: sb[f"vr{holder['i']}"][:, jb, :]),
                                           mask=None, first=False, last=(jb == 0), reads=_DynReads(holder), piece=pc, out=outfn, holder=holder))
        run_attention(blocks)

        if KD == "t_att":
            raise StopBuild()
        for half in range(2):
            sng, slg = load_slab(("in", 8 + half))
            for cc in range(4):
                pgn, pg = nps()
                mm_fm(pg, pgn, sng, slg, cc * 128, "xnT", xnT, nt)
                tr.op("act", [pgn], ["gel"], lambda e: e.activation(out=gel[:, half * 4 + cc, 0:nt], in_=pg[:, 0:nt], func=AF.Gelu_apprx_tanh))
        for half in range(2):
            snx, slx = load_slab(("in", 6 + half))
            for cc in range(4):
                c8 = half * 4 + cc
                xln, xl = ntf()
                xl3 = xl[:, 0:NSEG * (3 + SLEN)].rearrange("p (s l) -> p s l", s=NSEG)
                pn, p = nps()
                mm_fm(p, pn, snx, slx, cc * 128, "xnT", xnT, nt)
                if prompt:
                    tr.op("dve", ["hist"], [xln], lambda e: e.tensor_copy(out=xl3[:, 0, 0:3], in_=sb["hist"][:, c8, :]))
                else:
                    tr.op("dve", ["sconv"], [xln], lambda e: e.tensor_copy(out=xl3[:, :, 0:3], in_=sb["sconv"][:, c8, :, :]))
                tr.op("act", [pn], [xln], lambda e: e.copy(out=xl3[:, :, 3:3 + SLEN], in_=p[:, 0:nt].rearrange("p (s l) -> p s l", s=NSEG)))
                if prompt:
                    tr.op("dve", [xln], ["hist"], lambda e: e.tensor_copy(out=sb["hist"][:, c8, :], in_=xl3[:, 0, SLEN:SLEN + 3]))
                else:
                    tr.op("dve", [xln], ["csam"], lambda e: e.tensor_copy(out=sb["csam"][:, c8, :, :], in_=xl3[:, :, SLEN:SLEN + 3]))
                xcn, xc = ntf()
                xc3 = xc[:, 0:nt].rearrange("p (s l) -> p s l", s=NSEG)
                tr.op("dve", [xln, "pvec"], [xcn], lambda e: e.tensor_scalar(out=xc3, in0=xl3[:, :, 0:SLEN], scalar1=pvec[:, PV_CW + c8:PV_CW + c8 + 1], scalar2=pvec[:, PV_CB + c8:PV_CB + c8 + 1], op0=ALU.mult, op1=ALU.add))
                for i in range(1, 4):
                    tr.op("dve", [xln, "pvec", xcn], [xcn], lambda e: e.scalar_tensor_tensor(out=xc3, in0=xl3[:, :, i:i + SLEN], scalar=pvec[:, PV_CW + i * 8 + c8:PV_CW + i * 8 + c8 + 1], in1=xc3, op0=ALU.mult, op1=ALU.add))
                xbn, xcb = ntb()
                tr.op("dve", [xcn], [xbn], lambda e: e.tensor_copy(out=xcb[:, 0:nt], in_=xc[:, 0:nt]))
                prn, pr = nps()
                tr.op("pe", ["wa", xbn], [prn], lambda e: e.matmul(pr[:, 0:nt], lhsT=sb["wa"][:, c8, :], rhs=xcb[:, 0:nt], start=True, stop=True))
                pin, pi_ = nps()
                tr.op("pe", ["wx", xbn], [pin], lambda e: e.matmul(pi_[:, 0:nt], lhsT=sb["wx"][:, c8, :], rhs=xcb[:, 0:nt], start=True, stop=True))
                rn, r = ntf()
                tr.op("act", [prn, "pvec"], [rn], lambda e: e.activation(out=r[:, 0:nt], in_=pr[:, 0:nt], func=AF.Sigmoid, bias=pvec[:, PV_BA + c8:PV_BA + c8 + 1]))
                in_, ig = ntf()
                tr.op("act", [pin, "pvec"], [in_], lambda e: e.activation(out=ig[:, 0:nt], in_=pi_[:, 0:nt], func=AF.Sigmoid, bias=pvec[:, PV_BX + c8:PV_BX + c8 + 1]))
                an_, a = ntf()
                tr.op("act", [rn, "lrc0"], [an_], lambda e: e.activation(out=a[:, 0:nt], in_=r[:, 0:nt], func=AF.Exp, scale=lrc[:, 0, c8:c8 + 1]))
                a2n, a2 = ntf()
                tr.op("act", [rn, "lrc1"], [a2n], lambda e: e.activation(out=a2[:, 0:nt], in_=r[:, 0:nt], func=AF.Exp, scale=lrc[:, 1, c8:c8 + 1]))
                tr.op("act", [a2n], [a2n], lambda e: e.activation(out=a2[:, 0:nt], in_=a2[:, 0:nt], func=AF.Ln, scale=-1.0, bias=1.0))
                tr.op("act", [a2n], [a2n], lambda e: e.activation(out=a2[:, 0:nt], in_=a2[:, 0:nt], func=AF.Exp, scale=0.5))
                tr.op("dve", [in_, xcn], [in_], lambda e: e.tensor_tensor(out=ig[:, 0:nt], in0=ig[:, 0:nt], in1=xc[:, 0:nt], op=ALU.mult))
                tr.op("dve", [in_, a2n], [in_], lambda e: e.tensor_tensor(out=ig[:, 0:nt], in0=ig[:, 0:nt], in1=a2[:, 0:nt], op=ALU.mult))
                hn, hh_ = ntf()
                for sgi in range(NSEG):
                    init = sb["hcar"][:, c8:c8 + 1] if prompt else sb["slru"][:, c8, sgi:sgi + 1]
                    tr.op("dve", [an_, in_, "hcar", "slru"], [hn], lambda e: e.tensor_tensor_scan(out=hh_[:, sgi * SLEN:(sgi + 1) * SLEN], data0=a[:, sgi * SLEN:(sgi + 1) * SLEN], data1=ig[:, sgi * SLEN:(sgi + 1) * SLEN], initial=init, op0=ALU.mult, op1=ALU.add))
                if prompt:
                    tr.op("dve", [hn], ["hcar"], lambda e: e.tensor_copy(out=sb["hcar"][:, c8:c8 + 1], in_=hh_[:, nt - 1:nt]))
                else:
                    tr.op("dve", [hn], ["hsam"], lambda e: e.tensor_copy(out=sb["hsam"][:, c8, :], in_=hh_[:, 0:nt].rearrange("p (s l) -> p s l", s=NSEG)[:, :, SLEN - 1]))
                tr.op("dve", [hn, "gel"], ["olruT"], lambda e: e.tensor_tensor(out=sb["olruT"][:, c8, 0:nt], in0=hh_[:, 0:nt], in1=gel[:, c8, 0:nt], op=ALU.mult))

        if KD == "t_lru":
            raise StopBuild()
        nbat = 1 if prompt else SBN
        for half in range(2):
            snq, slq = load_slab(("in", 10 + half))
            for hh in range(2):
                hm = half * 2 + hh
                qfs = [("qf0", sb["qf0"]), ("qf1", sb["qf1"])]; sqs = [("sq0", sb["sq0"]), ("sq1", sb["sq1"])]
                for dc in range(2):
                    pn, p = nps()
                    mm_fm(p, pn, snq, slq, (hh * 2 + dc) * 128, "xnT", xnT, nt)
                    tr.op("act", [pn], [sqs[dc][0]], lambda e: e.activation(out=sqs[dc][1][:, 0:nt], in_=p[:, 0:nt], func=AF.Square))
                    tr.op("dve", [pn], [qfs[dc][0]], lambda e: e.tensor_copy(out=qfs[dc][1][:, 0:nt], in_=p[:, 0:nt]))
                pn, p = nps()
                for dc in range(2):
                    tr.op("pe", [sqs[dc][0], "cst"], [pn], lambda e: e.matmul(p[:, 0:nt], lhsT=ones, rhs=sqs[dc][1][:, 0:nt], start=(dc == 0), stop=(dc == 1)), signal=(dc == 1))
                rsn, rs = "rsq", sb["rsq"]
                tr.op("dve", [pn], [rsn], lambda e: e.tensor_scalar(out=rs[:, 0:nt], in0=p[:, 0:nt], scalar1=1.0 / MD, scalar2=EPS, op0=ALU.mult, op1=ALU.add))
                tr.op("act", [rsn], [rsn], lambda e: e.activation(out=rs[:, 0:nt], in_=rs[:, 0:nt], func=AF.Ln))
                tr.op("act", [rsn], [rsn], lambda e: e.activation(out=rs[:, 0:nt], in_=rs[:, 0:nt], func=AF.Exp, scale=-0.5))
                qns = []
                for dc in range(2):
                    qnn, qn = f"qn{dc}", sb[f"qn{dc}"]
                    tr.op("dve", [qfs[dc][0], rsn], [qnn], lambda e: e.tensor_tensor(out=qn[:, 0:nt], in0=qfs[dc][1][:, 0:nt], in1=rs[:, 0:nt], op=ALU.mult))
                    qns.append((qnn, qn))
                for bI in range(nbat):
                    if not prompt and hm == 0 and half == 0:
                        pass
                    c0, c1 = (0, nt) if prompt else (bI * SQ, (bI + 1) * SQ)
                    if not prompt:
                        load_sample_mem(bI, hm)
                    es = []
                    for ncI in range(2):
                        pn, p = nps()
                        for dc in range(2):
                            tr.op("pe", ["memKT", qns[dc][0]], [pn], lambda e: e.matmul(p[:, c0:c1], lhsT=sb["memKT"][:, hm if prompt else 0, dc, ncI * 128:(ncI + 1) * 128], rhs=qns[dc][1][:, c0:c1], start=(dc == 0), stop=(dc == 1)), signal=(dc == 1))
                        en, eb = ntb()
                        tr.op("act", [pn], [en], lambda e: e.activation(out=eb[:, c0:c1], in_=p[:, c0:c1], func=AF.Exp))
                        es.append((en, eb))
                    pn, p = nps()
                    for ncI in range(2):
                        tr.op("pe", [es[ncI][0], "cst"], [pn], lambda e: e.matmul(p[:, c0:c1], lhsT=ones, rhs=es[ncI][1][:, c0:c1], start=(ncI == 0), stop=(ncI == 1)), signal=(ncI == 1))
                    rcn, rc = ntf()
                    tr.op("dve", [pn], [rcn], lambda e: e.reciprocal(out=rc[:, c0:c1], in_=p[:, c0:c1]))
                    for dc in range(2):
                        pn, p = nps()
                        for ncI in range(2):
                            vcol = (hm if prompt else 0) * 256 + dc * 128
                            tr.op("pe", ["memV", es[ncI][0]], [pn], lambda e: e.matmul(p[:, c0:c1], lhsT=sb["memV"][:, ncI, vcol:vcol + 128], rhs=es[ncI][1][:, c0:c1], start=(ncI == 0), stop=(ncI == 1)), signal=(ncI == 1))
                        tr.op("dve", [pn, rcn], ["omemT"], lambda e: e.tensor_tensor(out=sb["omemT"][:, hm * 2 + dc, c0:c1], in0=p[:, c0:c1], in1=rc[:, c0:c1], op=ALU.mult))

        if KD == "t_mem":
            raise StopBuild()
        for jg in range(2):
            for b, (oname) in enumerate(["osbT", "olruT", "omemT"]):
                sng_, slg_ = load_slab(("in", 12 + b * 2 + jg))
                snb, slb = load_slab(("br", b, jg))
                oT = sb[oname]
                for jj in range(4):
                    j = jg * 4 + jj
                    pn, p = nps()
                    mm_fm(p, pn, sng_, slg_, jj * 128, "xnT", xnT, nt)
                    gn, gt = ntf()
                    bcol = PV_BM + b * 8 + j
                    tr.op("act", [pn, "pvec"], [gn], lambda e: e.activation(out=gt[:, 0:nt], in_=p[:, 0:nt], func=AF.Sigmoid, bias=pvec[:, bcol:bcol + 1]))
                    pn2, p2 = nps()
                    mm_fm(p2, pn2, snb, slb, jj * 128, oname, oT, nt)
                    an_ = f"acc{jj}"
                    accs = sb["accs"]
                    if b == 0:
                        tr.op("dve", [pn2, gn], [an_], lambda e: e.tensor_tensor(out=accs[:, jj, 0:nt], in0=p2[:, 0:nt], in1=gt[:, 0:nt], op=ALU.mult))
                    else:
                        tr.op("dve", [pn2, gn], [gn], lambda e: e.tensor_tensor(out=gt[:, 0:nt], in0=p2[:, 0:nt], in1=gt[:, 0:nt], op=ALU.mult))
                        if b == 1:
                            tr.op("dve", [gn, an_], [an_], lambda e: e.tensor_tensor(out=accs[:, jj, 0:nt], in0=accs[:, jj, 0:nt], in1=gt[:, 0:nt], op=ALU.add))
                        else:
                            tr.op("dve", [gn, an_], ["mergedT"], lambda e: e.tensor_tensor(out=sb["mergedT"][:, j, 0:nt], in0=accs[:, jj, 0:nt], in1=gt[:, 0:nt], op=ALU.add))

        if KD == "t_merge":
            raise StopBuild()
        mergedT = sb["mergedT"]
        for half in range(2):
            sno, slo = load_slab(("out", half))
            for c in range(4):
                pn, p = nps()
                for k in range(8):
                    tr.op("pe", [sno, "mergedT"], [pn], lambda e: e.matmul(p[0:CR, :], lhsT=mergedT[:, k, c * CR:(c + 1) * CR], rhs=slo[:, k, :], start=(k == 0), stop=(k == 7)), signal=(k == 7))
                tr.op("dve", [pn, "xt"], ["xt"], lambda e: e.tensor_tensor(out=xt[0:CR, c, half * 512:(half + 1) * 512], in0=xt[0:CR, c, half * 512:(half + 1) * 512], in1=p[0:CR, :], op=ALU.add))

        import os
        if os.environ.get("KDBG") == "noffn":
            if prompt:
                tr.dma(STQ, ["xt"], ["yp"], lambda e: e.dma_start(out=yp[t0:t0 + 512, :].rearrange("(c p) f -> p c f", p=128), in_=xt[:, :, :]), "xt")
            else:
                tr.dma(STQ, ["xt"], ["ys"], lambda e: e.dma_start(out=ys.rearrange("(c p) f -> p c f", p=CR), in_=xt[0:CR, :, :]), "xt")
            return
        rmsnorm_T("xt", xt, G_FFN, CR, 4, "xnT", xnT, nt)
        for j in range(6):
            sng_, slg_ = load_slab(("fg", j))
            snu, slu = load_slab(("fu", j))
            nf = 4 if j < 5 else 2
            for ff in range(nf):
                f = j * 4 + ff
                pn, p = nps()
                mm_fm(p, pn, sng_, slg_, ff * 128, "xnT", xnT, nt)
                pn2, p2 = nps()
                mm_fm(p2, pn2, snu, slu, ff * 128, "xnT", xnT, nt)
                sn_, s_ = ntf()
                tr.op("act", [pn], [sn_], lambda e: e.activation(out=s_[:, 0:nt], in_=p[:, 0:nt], func=AF.Silu))
                tr.op("dve", [pn2, sn_], ["hT"], lambda e: e.tensor_tensor(out=hT[:, f, 0:nt], in0=p2[:, 0:nt], in1=s_[:, 0:nt], op=ALU.mult))
        for half in range(2):
            pss = [nps() for _ in range(4)]
            for kg in range(3):
                snd, sld = load_slab(("fd", half, kg))
                kc = SL[("fd", half, kg)][3]
                for c in range(4):
                    pn, p = pss[c]
                    for kk in range(kc):
                        f = kg * 8 + kk
                        last = (kg == 2 and kk == kc - 1)
                        tr.op("pe", [snd, "hT"], [pn], lambda e: e.matmul(p[0:CR, :], lhsT=hT[:, f, c * CR:(c + 1) * CR], rhs=sld[:, kk, :], start=(f == 0), stop=last), signal=(kk == kc - 1))
            for c in range(4):
                pn, p = pss[c]
                tr.op("dve", [pn, "xt"], ["xt"], lambda e: e.tensor_tensor(out=xt[0:CR, c, half * 512:(half + 1) * 512], in0=xt[0:CR, c, half * 512:(half + 1) * 512], in1=p[0:CR, :], op=ALU.add))
        if prompt:
            tr.dma(STQ, ["xt"], ["yp"], lambda e: e.dma_start(out=yp[t0:t0 + 512, :].rearrange("(c p) f -> p c f", p=128), in_=xt[:, :, :]), "xt")
        else:
            tr.dma(STQ, ["xt"], ["ys"], lambda e: e.dma_start(out=ys.rearrange("(c p) f -> p c f", p=CR), in_=xt[0:CR, :, :]), "xt")

    class _DynReads(list):
        def __init__(self, holder):
            super().__init__()
            self.holder = holder

        def __iter__(self):
            i = self.holder["i"]
            return iter(["QT", f"kr{i}", f"vr{i}"])

        def __add__(self, other):
            return list(self) + list(other)

        def __radd__(self, other):
            return list(other) + list(self)

    loaded_mem = {}

    def load_sample_mem(bI, hm):
        tbn, tb_ = ntb()
        fn_, f_ = ntf()
        tr.dma("sp", [], [fn_], lambda e: e.dma_start(out=f_[:, 0:512].rearrange("p (c d) -> p c d", c=2), in_=cmk[bI, hm, :, :].rearrange("(c p) d -> p c d", p=128)), fn_)
        tr.op("dve", [fn_], [tbn], lambda e: e.tensor_copy(out=tb_[:, :], in_=f_[:, 0:512]))
        fn2_, f2_ = ntf()
        tr.dma("sp", [], [fn2_], lambda e: e.dma_start(out=f2_[:, 0:512].rearrange("p (c d) -> p c d", c=2), in_=cmv[bI, hm, :, :].rearrange("(c p) d -> p c d", p=128)), fn2_)
        tr.op("dve", [fn2_], ["memV"], lambda e: e.tensor_copy(out=sb["memV"][:, :, 0:256], in_=f2_[:, 0:512].rearrange("p (c d) -> p c d", c=2)))
        for dc in range(2):
            pn, p = npt()
            for c in range(2):
                tr.op("pe", [tbn, "cst"], [pn], lambda e: e.transpose(out=p[:, c * 128:(c + 1) * 128], in_=tb_[:, c * 256 + dc * 128:c * 256 + (dc + 1) * 128], identity=ident), signal=(c == 1))
            tr.op("dve", [pn, "gq16"], ["memKT"], lambda e: e.tensor_scalar(out=sb["memKT"][:, 0, dc, :], in0=p[:, 0:256], scalar1=sb["gq16"][:, dc:dc + 1], scalar2=None, op0=ALU.mult))

    salloc("accs", [128, 4, 512], F32)

    per_tile = [("in", j) for j in range(6)] + [("in", 8), ("in", 9), ("in", 6), ("in", 7), ("in", 10), ("in", 11)]
    for jg in range(2):
        for b in range(3):
            per_tile += [("in", 12 + b * 2 + jg), ("br", b, jg)]
    per_tile += [("out", 0), ("out", 1)]
    for j in range(6):
        per_tile += [("fg", j), ("fu", j)]
    for half in range(2):
        for kg in range(3):
            per_tile.append(("fd", half, kg))
    sched["list"] = per_tile * (NTILE + 1)
    if KD == "setup":
        tr.final_wait("sp")
        return nc
    mem_prompt_prep()
    if KD.startswith("mem"):
        tr.final_wait("sp")
        return nc
    try:
        for ti in range(NTILE):
            tile("p", ti)
    except StopBuild:
        tr.final_wait("sp")
        return nc
    tr.dma(STQ, ["hist"], ["cp"], lambda e: e.dma_start(out=cp, in_=sb["hist"][:]), "hist")
    tr.dma(STQ, ["hcar"], ["hp"], lambda e: e.dma_start(out=hp, in_=sb["hcar"][:]), "hcar")
    tile("s", 0)
    tr.dma(STQ, ["csam"], ["cs"], lambda e: e.dma_start(out=cs, in_=sb["csam"][:]), "csam")
    tr.dma(STQ, ["hsam"], ["hs"], lambda e: e.dma_start(out=hs, in_=sb["hsam"][:]), "hsam")
    tr.final_wait("sp")
    return nc


def _consts():
    c = np.zeros((128, NCST), np.float32)
    p = np.arange(128)[:, None]
    j = np.arange(128)[None, :]
    c[:, C_ID:C_ID + 128] = (p == j)
    c[:, C_NTRI:C_NTRI + 128] = -1.0 * (p >= j)
    c[:, C_NONE:C_NONE + 128] = -1.0
    c[:, C_ONE:C_ONE + 128] = 1.0
    t = np.arange(512)[None, :]
    for jb in range(4):
        c[:, C_MASK + jb * 512:C_MASK + (jb + 1) * 512] = (t > 128 * jb + p)
    return c


def _fm(v):
    v = np.asarray(v, np.float32)
    lead = v.shape[:-1]
    c = v.shape[-1] // 128
    v = v.reshape(lead + (c, 128))
    return np.ascontiguousarray(np.moveaxis(v, -1, 0))


_CACHE = {}


def run(inputs, T, PAST, SBN, SQ, ncores):
    key = (T, PAST, SBN, SQ)
    if key not in _CACHE:
        _CACHE[key] = build(T, PAST, SBN, SQ)
    nc = _CACHE[key]
    f = lambda a: np.ascontiguousarray(np.asarray(a, np.float32))
    pv = np.zeros((128, NPV), np.float32)
    pv[:, PV_BM:PV_BM + 24] = _fm(inputs["b_merge"][0])
    pv[:, PV_CW:PV_CW + 32] = _fm(inputs["conv_w"][0]).reshape(128, 32)
    pv[:, PV_CB:PV_CB + 8] = _fm(inputs["conv_b"][0])
    pv[:, PV_BA:PV_BA + 8] = _fm(inputs["lru_ba"][0])
    pv[:, PV_BX:PV_BX + 8] = _fm(inputs["lru_bx"][0])
    pv[:, PV_AL:PV_AL + 8] = _fm(inputs["lru_a_logit"][0])
    pv[:, PV_GQ:PV_GQ + 2] = _fm(inputs["q_norm_g"][0])
    bv = np.concatenate([f(inputs["norm_mix_g"][0]), f(inputs["norm_ffn_g"][0]), f(inputs["mem_norm_g"][0]), f(inputs["k_norm_g"][0])])[None, :]
    cst = _consts()
    shared = {"pvec": pv, "bvec": np.ascontiguousarray(bv), "cst": cst,
              "lru_wa": f(inputs["lru_wa"][0]), "lru_wx": f(inputs["lru_wx"][0])}
    for w in ["w_in", "w_mem_k", "w_mem_v", "w_br_sb", "w_br_lru", "w_br_mem", "w_out", "w_ffn_gate", "w_ffn_up", "w_ffn_down"]:
        shared[w] = f(inputs[w][0])
    in_maps = []
    for c in range(ncores):
        sl = slice(c * SBN, (c + 1) * SBN)
        m = dict(shared)
        m["xp"] = f(inputs["x_prompt"][c])
        m["xs"] = f(inputs["x_sample"][sl]).reshape(SBN * SQ, D)
        m["memp"] = f(inputs["mem_prompt"][c])
        m["csk"] = f(inputs["cache_sb_k"][0, sl]); m["csv"] = f(inputs["cache_sb_v"][0, sl])
        m["sconv"] = np.ascontiguousarray(_fm(inputs["state_conv"][0, sl]))
        m["sconv"] = np.ascontiguousarray(m["sconv"].transpose(0, 3, 1, 2))
        m["slru"] = np.ascontiguousarray(_fm(inputs["state_lru_h"][0, sl]).transpose(0, 2, 1))
        m["cmk"] = f(inputs["cache_mem_k"][0, sl]); m["cmv"] = f(inputs["cache_mem_v"][0, sl])
        in_maps.append(m)
    res = run_bass_kernel_spmd(nc, in_maps, core_ids=list(range(ncores)))
    R = res.results

    def unfm(a):
        a = np.asarray(a)
        return np.moveaxis(a, (0, 1), (-1, -2)).reshape(a.shape[2:] + (1024,))

    yp = np.stack([R[c]["yp"] for c in range(ncores)])
    ys = np.concatenate([np.asarray(R[c]["ys"]).reshape(SBN, SQ, D) for c in range(ncores)])
    kp = np.stack([np.asarray(R[c]["kp"]).transpose(0, 2, 1) for c in range(ncores)])[None]
    vp = np.stack([R[c]["vp"] for c in range(ncores)])[None]
    cpo = np.stack([unfm(R[c]["cp"]) for c in range(ncores)])[None]
    hpo = np.stack([unfm(R[c]["hp"]) for c in range(ncores)])[None]
    mkp = np.stack([R[c]["mkp"] for c in range(ncores)])[None]
    mvp = np.stack([R[c]["mvp"] for c in range(ncores)])[None]
    ksn = np.concatenate([np.asarray(R[c]["ksn"]).reshape(H, DH, SBN, SQ).transpose(2, 0, 3, 1) for c in range(ncores)])[None]
    vsn = np.concatenate([R[c]["vsn"] for c in range(ncores)])[None]
    cso = np.concatenate([unfm(R[c]["cs"]) for c in range(ncores)])[None]
    hso = np.concatenate([unfm(R[c]["hs"]) for c in range(ncores)])[None]
    outs = (yp, ys, kp, vp, cpo, hpo, mkp, mvp, ksn, vsn, cso, hso)
    return tuple(np.ascontiguousarray(o, dtype=np.float32) for o in outs)


def kernel(**inputs):
    return run(inputs, 8192, 1024, 4, 64, NCORES)
```

```python
import numpy as np
import concourse.bass as bass
import concourse.mybir as mybir
from concourse.bass_utils import run_bass_kernel_spmd

F32 = mybir.dt.float32
BF16 = mybir.dt.bfloat16
AF = mybir.ActivationFunctionType
ALU = mybir.AluOpType

D = 1024
H = 8
DH = 128
MH = 4
MD = 256
MT = 256
FF = 2816
INW = 9216
EPS = 1e-6
NCORES = 8

PV_BM = 0
PV_CW = 24
PV_CB = 56
PV_BA = 64
PV_BX = 72
PV_AL = 80
PV_GQ = 88
NPV = 90
C_ID = 0
C_NTRI = 128
C_NONE = 256
C_ONE = 384
C_MASK = 512
NCST = 512 + 4 * 512


class StopBuild(Exception):
    pass


class Tracker:
    ROT = 30000

    def __init__(self, nc):
        self.nc = nc
        self.eng = {"pe": nc.tensor, "act": nc.scalar, "dve": nc.vector,
                    "pool": nc.gpsimd, "sp": nc.sync}
        self.sems = {}
        self.epoch = {e: 0 for e in ("pe", "act", "dve", "pool")}
        self.cnt = {e: 0 for e in ("pe", "act", "dve", "pool")}
        for e in self.cnt:
            self.sems[(e, 0)] = nc.alloc_semaphore(f"s_{e}_0")
        self.seen = {e: {} for e in self.eng}
        self.buf = {}
        self.dcnt = {}
        self.alias = {}
        self.nwait = 0
        self.pending = {}
        self.final = {}

    def _st(self, name):
        s = self.buf.get(name)
        if s is None:
            s = {"w": None, "r": {}}
            self.buf[name] = s
        return s

    def _names(self, name):
        return [name] + self.alias.get(name, [])

    def _deps(self, reads, writes):
        deps = {}

        def add(t):
            if t is None:
                return
            k, v = t
            if deps.get(k, 0) < v:
                deps[k] = v

        for r in reads:
            for n in self._names(r):
                add(self._st(n)["w"])
                if n[:2] in ("ps", "pt"):
                    for k, v in self._st(n)["r"].items():
                        add((k, v))
        for w in writes:
            for n in self._names(w):
                s = self._st(n)
                add(s["w"])
                for k, v in s["r"].items():
                    add((k, v))
        return deps

    def _emit_waits(self, e, deps):
        seen = self.seen[e]
        for k, v in deps.items():
            if e == "pe" and k[0] == "pe":
                continue
            if seen.get(k, 0) >= v:
                continue
            if k[0] in self.cnt and k[1] == self.epoch[k[0]]:
                assert self.cnt[k[0]] >= v, f"wait on unsignalled ticket {k} {v}"
            self.eng[e].wait_ge(self.sems[k], v)
            seen[k] = v
            self.nwait += 1

    def _record(self, reads, writes, t):
        for r in reads:
            s = self._st(r)
            if s["r"].get(t[0], 0) < t[1]:
                s["r"][t[0]] = t[1]
        for w in writes:
            s = self._st(w)
            s["w"] = t
            s["r"] = {}

    def op(self, e, reads, writes, fn, signal=True):
        if self.cnt[e] >= self.ROT and not self.pending.get(e, False):
            self.final[(e, self.epoch[e])] = self.cnt[e]
            self.epoch[e] += 1
            self.cnt[e] = 0
            self.sems[(e, self.epoch[e])] = self.nc.alloc_semaphore(f"s_{e}_{self.epoch[e]}")
        self._emit_waits(e, self._deps(reads, writes))
        ins = fn(self.eng[e])
        if signal:
            self.cnt[e] += 1
            key = (e, self.epoch[e])
            ins.then_inc(self.sems[key], 1)
            t = (key, self.cnt[e])
            self.pending[e] = False
        else:
            t = ((e, self.epoch[e]), self.cnt[e] + 1)
            self.pending[e] = True
        self._record(reads, writes, t)
        return ins

    def dma(self, q, reads, writes, fn, slot):
        self._emit_waits(q, self._deps(reads, writes))
        key = ("dma", slot + "_" + q)
        if key not in self.sems:
            self.sems[key] = self.nc.alloc_semaphore(f"d_{slot}_{q}")
            self.dcnt[key] = 0
        ins = fn(self.eng[q])
        self.dcnt[key] += 16
        ins.then_inc(self.sems[key], 16)
        self._record(reads, writes, (key, self.dcnt[key]))
        return ins

    def final_wait(self, e):
        deps = {}
        for k in self.sems:
            if k[0] == "dma":
                deps[k] = self.dcnt[k]
            else:
                deps[k] = self.cnt[k[0]] if k[1] == self.epoch[k[0]] else self.final[k]
        self._emit_waits(e, deps)


def build(T, PAST, SBN=4, SQ=64):
    import os
    KD = os.environ.get("KDBG", "")
    STQ = os.environ.get("KSTQ", "sp")
    nc = bass.Bass("TRN2", target_bir_lowering=False)
    tr = Tracker(nc)
    NTILE = T // 512
    NPB = PAST // 128

    def din(name, shape, dt=F32):
        return nc.dram_tensor(name, list(shape), dt, kind="ExternalInput").ap()

    def dout(name, shape, dt=F32):
        return nc.dram_tensor(name, list(shape), dt, kind="ExternalOutput").ap()

    xp = din("xp", [T, D]); xs = din("xs", [SBN * SQ, D]); memp = din("memp", [MT, D])
    csk = din("csk", [SBN, H, PAST, DH]); csv = din("csv", [SBN, H, PAST, DH])
    sconv = din("sconv", [128, 8, SBN, 3]); slru = din("slru", [128, 8, SBN])
    cmk = din("cmk", [SBN, MH, MT, MD]); cmv = din("cmv", [SBN, MH, MT, MD])
    pvec_d = din("pvec", [128, NPV]); bvec_d = din("bvec", [1, 3 * D + MD]); cst_d = din("cst", [128, NCST])
    W = {
        "w_in": din("w_in", [D, INW]), "w_mem_k": din("w_mem_k", [D, D]), "w_mem_v": din("w_mem_v", [D, D]),
        "w_br_sb": din("w_br_sb", [D, D]), "w_br_lru": din("w_br_lru", [D, D]), "w_br_mem": din("w_br_mem", [D, D]),
        "w_out": din("w_out", [D, D]), "w_ffn_gate": din("w_ffn_gate", [D, FF]), "w_ffn_up": din("w_ffn_up", [D, FF]),
        "w_ffn_down": din("w_ffn_down", [FF, D]),
    }
    lru_wa_d = din("lru_wa", [8, 128, 128]); lru_wx_d = din("lru_wx", [8, 128, 128])

    yp = dout("yp", [T, D]); ys = dout("ys", [SBN * SQ, D])
    kp = dout("kp", [H, DH, T]); vp = dout("vp", [H, T, DH])
    cp = dout("cp", [128, 8, 3]); hp = dout("hp", [128, 8])
    mkp = dout("mkp", [MH, MT, MD]); mvp = dout("mvp", [MH, MT, MD])
    ksn = dout("ksn", [H, DH, SBN * SQ]); vsn = dout("vsn", [SBN, H, SQ, DH])
    cs = dout("cs", [128, 8, SBN, 3]); hs = dout("hs", [128, 8, SBN])

    SL = {}

    def addslab(key, w, r0, kc, c0, ncols):
        SL[key] = (len(SL), w, r0, kc, c0, ncols)

    for j in range(18):
        addslab(("in", j), "w_in", 0, 8, j * 512, 512)
    for b, nm in enumerate(["w_br_sb", "w_br_lru", "w_br_mem"]):
        for j in range(2):
            addslab(("br", b, j), nm, 0, 8, j * 512, 512)
    for j in range(2):
        addslab(("out", j), "w_out", 0, 8, j * 512, 512)
    for j in range(6):
        addslab(("fg", j), "w_ffn_gate", 0, 8, j * 512, min(512, FF - j * 512))
        addslab(("fu", j), "w_ffn_up", 0, 8, j * 512, min(512, FF - j * 512))
    for half in range(2):
        for kg in range(3):
            addslab(("fd", half, kg), "w_ffn_down", kg * 1024, min(8, 22 - kg * 8), half * 512, 512)
    for j in range(2):
        addslab(("mk", j), "w_mem_k", 0, 8, j * 512, 512)
        addslab(("mv", j), "w_mem_v", 0, 8, j * 512, 512)
    wscr = nc.dram_tensor("wscr", [len(SL), 128, 8, 512], BF16, kind="Internal").ap()
    ktscr = nc.dram_tensor("ktscr", [H, DH, T], BF16, kind="Internal").ap()
    vscr = nc.dram_tensor("vscr", [H, 128, T // 128, DH], BF16, kind="Internal").ap()

    sb = {}

    def salloc(name, shape, dt):
        sb[name] = nc.alloc_sbuf_tensor("sb_" + name, list(shape), dt)
        return sb[name]

    salloc("cst", [128, NCST], BF16)
    salloc("pvec", [128, NPV], F32)
    salloc("bvec", [128, 3 * D + MD], F32)
    salloc("lrc", [128, 4, 8], F32)
    salloc("gq16", [128, 2], F32)
    salloc("wa", [128, 8, 128], BF16); salloc("wx", [128, 8, 128], BF16)
    salloc("hist", [128, 8, 3], F32); salloc("hcar", [128, 8], F32)
    salloc("csam", [128, 8, SBN, 3], F32); salloc("hsam", [128, 8, SBN], F32)
    salloc("sconv", [128, 8, SBN, 3], F32); salloc("slru", [128, 8, SBN], F32)
    salloc("xt", [128, 4, D], F32)
    salloc("xnb", [128, 4, D], BF16)
    salloc("Vbf", [128, 4, D], BF16)
    salloc("xnT", [128, 8, 512], BF16)
    big = salloc("big", [128, 24, 512], BF16)
    salloc("olruT", [128, 8, 512], BF16)
    salloc("osbT", [128, 8, 512], BF16)
    sb["omemT"] = sb["Vbf"][:, :, :].rearrange("p c (a f) -> p (c a) f", f=512)
    sb["mergedT"] = sb["xnb"][:, :, :].rearrange("p c (a f) -> p (c a) f", f=512)
    NWR = 3
    for i in range(NWR):
        salloc(f"wr{i}", [128, 8, 512], BF16)
    NKR = 3
    for i in range(NKR):
        salloc(f"kr{i}", [128, 1024], BF16); salloc(f"vr{i}", [128, 8, 128], BF16)
    salloc("memKT", [128, MH, 2, MT], BF16); salloc("memV", [128, 2, D], BF16)
    salloc("stat", [128, 16], F32); salloc("mstat", [128, 4], F32)
    for nm_ in ("qf0", "qf1", "rsq"):
        salloc(nm_, [128, 512], F32)
    for nm_ in ("sq0", "sq1", "qn0", "qn1"):
        salloc(nm_, [128, 512], BF16)
    NTF = 8
    for i in range(NTF):
        salloc(f"tf{i}", [128, 520], F32)
    NTB = 10
    for i in range(NTB):
        salloc(f"tb{i}", [128, 512], BF16)
    for i in range(2):
        salloc(f"R32_{i}", [128, 512], F32); salloc(f"R16_{i}", [128, 512], BF16)
    tr.alias = {"hT": ["QT", "KTn", "gel"], "QT": ["hT"], "KTn": ["hT"], "gel": ["hT"],
                "omemT": ["Vbf"], "Vbf": ["omemT"], "mergedT": ["xnb"], "xnb": ["mergedT"]}
    QT = big[:, 0:8, :]; KTn = big[:, 8:16, :]; gel = big[:, 16:24, :]; hT = big[:, 0:22, :]

    ps = [nc.alloc_psum_tensor(f"ps{i}", [128, 512], F32) for i in range(6)]
    pt = [nc.alloc_psum_tensor(f"pt{i}", [128, 1024], BF16) for i in range(2)]
    rr = {"ps": 0, "pt": 0, "tf": 0, "tb": 0, "wr": 0, "kr": 0}

    def nxt(kind, n):
        i = rr[kind] % n
        rr[kind] = (i + 1) % n
        return i

    def nps(n=6):
        i = nxt("ps", n)
        return f"ps{i}", ps[i]

    def npt():
        i = nxt("pt", 2)
        return f"pt{i}", pt[i]

    def ntf():
        i = nxt("tf", NTF)
        return f"tf{i}", sb[f"tf{i}"]

    def ntb():
        i = nxt("tb", NTB)
        return f"tb{i}", sb[f"tb{i}"]

    cst = sb["cst"]
    ident = cst[:, C_ID:C_ID + 128]
    ntri = cst[:, C_NTRI:C_NTRI + 128]
    none_ = cst[:, C_NONE:C_NONE + 128]
    ones = cst[:, C_ONE:C_ONE + 128]
    pvec = sb["pvec"]; bvec = sb["bvec"]
    G_MIX = bvec[:, 0:D]; G_FFN = bvec[:, D:2 * D]; G_MEM = bvec[:, 2 * D:3 * D]; G_K = bvec[:, 3 * D:3 * D + MD]

    tr.dma("sp", [], ["pvec"], lambda e: e.dma_start(out=pvec[:], in_=pvec_d), "pvec")
    tr.dma("sp", [], ["bvec"], lambda e: e.dma_start(out=bvec[:], in_=bvec_d[0:1, :].partition_broadcast(128)), "bvec")
    tr.dma("sp", [], ["sconv"], lambda e: e.dma_start(out=sb["sconv"][:], in_=sconv), "sconv")
    tr.dma("sp", [], ["slru"], lambda e: e.dma_start(out=sb["slru"][:], in_=slru), "slru")
    for wn_, wd_ in (("wa", lru_wa_d), ("wx", lru_wx_d)):
        for n0 in (0, 4):
            fn_, f_ = ntf()
            tr.dma("sp", [], [fn_], lambda e: e.dma_start(out=f_[:, 0:512].rearrange("p (n e) -> p n e", n=4), in_=wd_[n0:n0 + 4].rearrange("n d e -> d n e")), fn_)
            tr.op("dve", [fn_], [wn_], lambda e: e.tensor_copy(out=sb[wn_][:, n0:n0 + 4, :], in_=f_[:, 0:512].rearrange("p (n e) -> p n e", n=4)))
    for ci in range(NCST // 512):
        fn_, f_ = ntf()
        tr.dma("sp", [], [fn_], lambda e: e.dma_start(out=f_[:, 0:512], in_=cst_d[:, ci * 512:(ci + 1) * 512]), fn_)
        tr.op("dve", [fn_], ["cst"], lambda e: e.tensor_copy(out=cst[:, ci * 512:(ci + 1) * 512], in_=f_[:, 0:512]))
    tr.op("dve", [], ["hist"], lambda e: e.memset(sb["hist"][:], 0.0))
    tr.op("dve", [], ["hcar"], lambda e: e.memset(sb["hcar"][:], 0.0))
    lrc = sb["lrc"]
    tr.op("act", ["pvec"], ["lrc2"], lambda e: e.activation(out=lrc[:, 2, :], in_=pvec[:, PV_AL:PV_AL + 8], func=AF.Exp, scale=-1.0))
    tr.op("act", ["lrc2"], ["lrc3"], lambda e: e.activation(out=lrc[:, 3, :], in_=lrc[:, 2, :], func=AF.Ln, bias=1.0))
    tr.op("dve", ["lrc3"], ["lrc0"], lambda e: e.tensor_scalar(out=lrc[:, 0, :], in0=lrc[:, 3, :], scalar1=-8.0, scalar2=None, op0=ALU.mult))
    tr.op("dve", ["lrc3"], ["lrc1"], lambda e: e.tensor_scalar(out=lrc[:, 1, :], in0=lrc[:, 3, :], scalar1=-16.0, scalar2=None, op0=ALU.mult))
    tr.op("dve", ["pvec"], ["gq16"], lambda e: e.tensor_scalar(out=sb["gq16"][:], in0=pvec[:, PV_GQ:PV_GQ + 2], scalar1=1.0 / 16.0, scalar2=None, op0=ALU.mult))

    touched = set()

    sched = {"list": [], "pos": 0}
    pending = {}

    def load_slab(key, scratch=True):
        if key in pending:
            res = pending.pop(key)
        else:
            res = issue_load(key, scratch)
        L = sched["list"]
        try:
            i = L.index(key, sched["pos"])
        except ValueError:
            i = None
        if i is not None:
            sched["pos"] = i + 1
            if i + 1 < len(L) and L[i + 1] not in pending:
                pending[L[i + 1]] = issue_load(L[i + 1], True)
        return res

    def issue_load(key, scratch=True):
        idx, wname, r0, kc, c0, ncols = SL[key]
        i = nxt("wr", NWR)
        slot = f"wr{i}"
        t = sb[slot]
        if key not in touched:
            touched.add(key)
            for k in range(kc):
                fn_, f_ = ntf()
                tr.dma("sp", [], [fn_], lambda e: e.dma_start(out=f_[:, 0:ncols], in_=W[wname][r0 + k * 128:r0 + (k + 1) * 128, c0:c0 + ncols]), fn_)
                tr.op("dve", [fn_], [slot], lambda e: e.tensor_copy(out=t[:, k, 0:ncols], in_=f_[:, 0:ncols]))
            if scratch:
                tr.dma(STQ, [slot], [f"wscr{idx}"], lambda e: e.dma_start(out=wscr[idx, :, 0:kc, 0:ncols], in_=t[:, 0:kc, 0:ncols]), slot)
        else:
            tr.dma("sp", [f"wscr{idx}"], [slot], lambda e: e.dma_start(out=t[:, 0:kc, 0:ncols], in_=wscr[idx, :, 0:kc, 0:ncols]), slot)
        return slot, t

    def rmsnorm_T(src_name, src, g_ap, CR, nch, dstT_name, dstT, ncolsT):
        stat = sb["stat"]
        tr.op("dve", [], ["stat_a"], lambda e: e.memset(stat[:, 0:4], 0.0))
        for c in range(nch):
            tr.op("act", [src_name, "stat_a"], ["xnb", "stat_a"], lambda e: e.activation(out=sb["xnb"][0:CR, c, :], in_=src[0:CR, c, :], func=AF.Square, accum_out=stat[0:CR, c:c + 1]))
        tr.op("dve", ["stat_a"], ["stat_b"], lambda e: e.tensor_scalar(out=stat[0:CR, 4:4 + nch], in0=stat[0:CR, 0:nch], scalar1=1.0 / D, scalar2=EPS, op0=ALU.mult, op1=ALU.add))
        tr.op("act", ["stat_b"], ["stat_c"], lambda e: e.activation(out=stat[0:CR, 8:8 + nch], in_=stat[0:CR, 4:4 + nch], func=AF.Ln))
        tr.op("act", ["stat_c"], ["stat_d"], lambda e: e.activation(out=stat[0:CR, 12:12 + nch], in_=stat[0:CR, 8:8 + nch], func=AF.Exp, scale=-0.5))
        for c in range(nch):
            tr.op("dve", [src_name, "stat_d", "bvec"], ["xnb"], lambda e: e.scalar_tensor_tensor(
                out=sb["xnb"][0:CR, c, :], in0=src[0:CR, c, :], scalar=stat[0:CR, 12 + c:13 + c], in1=g_ap[0:CR, :], op0=ALU.mult, op1=ALU.mult))
        for k in range(8):
            pn, p = npt()
            for c in range(nch):
                tr.op("pe", ["xnb", "cst"], [pn], lambda e: e.transpose(out=p[:, c * CR:(c + 1) * CR], in_=sb["xnb"][0:CR, c, k * 128:(k + 1) * 128], identity=ident[0:CR, 0:CR]), signal=(c == nch - 1))
            eng = "act" if k % 2 == 0 else "dve"
            if eng == "act":
                tr.op("act", [pn], [dstT_name], lambda e: e.copy(out=dstT[:, k, 0:ncolsT], in_=p[:, 0:ncolsT]))
            else:
                tr.op("dve", [pn], [dstT_name], lambda e: e.tensor_copy(out=dstT[:, k, 0:ncolsT], in_=p[:, 0:ncolsT]))

    def mm_fm(out_ps, pn, slab_name, slab, col, rhs_name, rhsT, nt, kc=8, extra_reads=()):
        for k in range(kc):
            tr.op("pe", [slab_name, rhs_name] + list(extra_reads), [pn], lambda e: e.matmul(out_ps[:, 0:nt], lhsT=slab[:, k, col:col + 128], rhs=rhsT[:, k, 0:nt], start=(k == 0), stop=(k == kc - 1)), signal=(k == kc - 1))

    class Piece:
        def __init__(self, fn):
            self.fn = fn
            self.done = False

        def ensure(self):
            if not self.done:
                self.done = True
                self.fn(self)

    def run_attention(blocks):
        n = len(blocks)
        LOOK = 6
        st = {}
        grp = {"i": 0}

        def S1(b):
            zn, z = nps(4)
            b["zn"], b["z"] = zn, z
            nk, nq = b["nk"], b["nq"]
            kT = b["kT"](); b["vap"] = b["v"]()
            tr.op("pe", b["reads"], [zn], lambda e: e.matmul(z[0:nk, 0:nq], lhsT=kT, rhs=b["q"], start=True, stop=True))
            en, E = ntf()
            tr.op("act", [zn], [en], lambda e: e.activation(out=E[0:nk, 0:nq], in_=z[0:nk, 0:nq], func=AF.Exp))
            b["en"], b["E"] = en, E

        def S1b(b):
            nk, nq = b["nk"], b["nq"]
            en, E = b["en"], b["E"]
            sn, SP = ntb()
            b["sn"], b["SP"] = sn, SP
            tr.op("act", [en], [sn], lambda e: e.activation(out=SP[0:nk, 0:nq], in_=E[0:nk, 0:nq], func=AF.Ln, bias=1.0))
            if b["mask"] is not None:
                m = b["mask"]
                tr.op("dve", [sn, "cst"], [sn], lambda e: e.tensor_tensor(out=SP[0:nk, 0:nq], in0=SP[0:nk, 0:nq], in1=m, op=ALU.mult))

        def S2(b):
            nk, nq = b["nk"], b["nq"]
            c0 = b.get("c0", 0)
            z, zn, SP, sn = b["z"], b["zn"], b["SP"], b["sn"]
            if b["first"]:
                grp["i"] ^= 1
                g = grp["i"]
                tr.op("dve", [], [f"R32_{g}"], lambda e: e.memset(sb[f"R32_{g}"][:, 0:c0 + nq], 0.0))
            g = grp["i"]
            R32n, R16n = f"R32_{g}", f"R16_{g}"
            R32, R16 = sb[R32n], sb[R16n]
            tr.op("pe", [sn, "cst"], [zn], lambda e: e.matmul(z[0:nk, 0:nq], lhsT=ntri[0:nk, 0:nk], rhs=SP[0:nk, 0:nq], start=False, stop=b["first"], skip_group_check=True), signal=b["first"])
            if not b["first"]:
                tr.op("pe", [R16n, "cst"], [zn], lambda e: e.matmul(z[0:nk, 0:nq], lhsT=none_[:, 0:nk], rhs=R16[:, c0:c0 + nq], start=False, stop=True, skip_group_check=True))
            an, A = ntb()
            b["an"], b["A"] = an, A
            tr.op("act", [zn], [an], lambda e: e.activation(out=A[0:nk, 0:nq], in_=z[0:nk, 0:nq], func=AF.Exp))
            if b["mask"] is not None:
                m = b["mask"]
                tr.op("dve", [an, "cst"], [an], lambda e: e.tensor_tensor(out=A[0:nk, 0:nq], in0=A[0:nk, 0:nq], in1=m, op=ALU.mult))
            if not b["last"]:
                tr.op("dve", [sn, R32n], [R32n], lambda e: e.tensor_tensor(out=R32[0:nk, c0:c0 + nq], in0=R32[0:nk, c0:c0 + nq], in1=SP[0:nk, 0:nq], op=ALU.add))
                tr.op("dve", [R32n], [R16n], lambda e: e.tensor_copy(out=R16[:, 0:c0 + nq], in_=R32[:, 0:c0 + nq]))

        def S3(b):
            nk, nq = b["nk"], b["nq"]
            if b["first"]:
                st["on"], st["o"] = f"ps{4 + (st.get('oi', 0))}", ps[4 + st.get("oi", 0)]
                st["oi"] = 1 - st.get("oi", 0)
            on, o = st["on"], st["o"]
            vap, A, an = b["vap"], b["A"], b["an"]
            c0 = b.get("c0", 0)
            tr.op("pe", [an] + b["reads"], [on], lambda e: e.matmul(o[:, c0:c0 + nq], lhsT=vap, rhs=A[0:nk, 0:nq], start=b["first"], stop=b["last"], skip_group_check=True), signal=b["last"])
            if b["last"]:
                b["out"](on, o)

        for i in range(n + 3):
            if i < n:
                for j in range(i, min(n, i + LOOK)):
                    if blocks[j]["piece"] is not None:
                        blocks[j]["piece"].ensure()
                S1(blocks[i])
            if 0 <= i - 2 < n:
                S2(blocks[i - 2])
            if i < n:
                S1b(blocks[i])
            if 0 <= i - 3 < n:
                S3(blocks[i - 3])

    def mem_prompt_prep():
        mt = sb["xt"]
        tr.dma("sp", [], ["xt"], lambda e: e.dma_start(out=mt[:, 0:2, :], in_=memp.rearrange("(c p) f -> p c f", p=128)), "xt")
        rmsnorm_T("xt", mt, G_MEM, 128, 2, "xnT", sb["xnT"], 256)
        if KD == "mem1":
            return
        mT = sb["xnT"]
        kf = sb["Vbf"]
        for which in ("mk", "mv"):
            if KD == "mem8" and which == "mv":
                return
            for half in range(2):
                sn_, slab = load_slab((which, half), scratch=False)
                if KD == "mem2":
                    return
                for c in range(2):
                    pn, p = nps()
                    for k in range(8):
                        tr.op("pe", [sn_, "xnT"], [pn], lambda e: e.matmul(p[:, :], lhsT=mT[:, k, c * 128:(c + 1) * 128], rhs=slab[:, k, :], start=(k == 0), stop=(k == 7)), signal=(k == 7))
                    if KD == "mem3":
                        return
                    for hh in range(2):
                        hm = half * 2 + hh
                        fn, f = ntf()
                        if which == "mv":
                            tr.op("act", [pn], [fn], lambda e: e.copy(out=f[:, 0:256], in_=p[:, hh * 256:(hh + 1) * 256]))
                            tr.op("dve", [pn], ["memV"], lambda e: e.tensor_copy(out=sb["memV"][:, c, hm * 256:(hm + 1) * 256], in_=p[:, hh * 256:(hh + 1) * 256]))
                            tr.dma(STQ, [fn], ["mvp"], lambda e: e.dma_start(out=mvp[hm, c * 128:(c + 1) * 128, :], in_=f[:, 0:256]), fn)
                        else:
                            stat = sb["mstat"]
                            jn, jk = ntf()
                            tr.op("dve", [], ["ms_a"], lambda e: e.memset(stat[:, 0:1], 0.0))
                            tr.op("act", [pn, "ms_a"], [jn, "ms_a"], lambda e: e.activation(out=jk[:, 0:256], in_=p[:, hh * 256:(hh + 1) * 256], func=AF.Square, accum_out=stat[:, 0:1]))
                            if KD == "mem4":
                                return
                            tr.op("dve", ["ms_a"], ["ms_b"], lambda e: e.tensor_scalar(out=stat[:, 1:2], in0=stat[:, 0:1], scalar1=1.0 / MD, scalar2=EPS, op0=ALU.mult, op1=ALU.add))
                            tr.op("act", ["ms_b"], ["ms_c"], lambda e: e.activation(out=stat[:, 2:3], in_=stat[:, 1:2], func=AF.Ln))
                            tr.op("act", ["ms_c"], ["ms_d"], lambda e: e.activation(out=stat[:, 3:4], in_=stat[:, 2:3], func=AF.Exp, scale=-0.5))
                            tr.op("dve", [pn, "ms_d", "bvec", jn], [fn], lambda e: e.scalar_tensor_tensor(out=f[:, 0:256], in0=p[:, hh * 256:(hh + 1) * 256], scalar=stat[:, 3:4], in1=G_K, op0=ALU.mult, op1=ALU.mult))
                            if KD == "mem5":
                                return
                            tr.dma(STQ, [fn], ["mkp"], lambda e: e.dma_start(out=mkp[hm, c * 128:(c + 1) * 128, :], in_=f[:, 0:256]), fn)
                            if KD == "mem6":
                                return
                            tr.op("dve", [fn], ["Vbf"], lambda e: e.tensor_copy(out=kf[:, c, hm * 256:(hm + 1) * 256], in_=f[:, 0:256]))
                            if KD == "mem7":
                                return
        if KD == "mem9":
            return
        memKT_from_tm(kf, "Vbf")

    def memKT_from_tm(ktm, ktm_name):
        for hm in range(MH):
            for dc in range(2):
                pn, p = npt()
                for c in range(2):
                    tr.op("pe", [ktm_name, "cst"], [pn], lambda e: e.transpose(out=p[:, c * 128:(c + 1) * 128], in_=ktm[:, c, hm * 256 + dc * 128: hm * 256 + (dc + 1) * 128], identity=ident), signal=(c == 1))
                tr.op("dve", [pn, "gq16"], ["memKT"], lambda e: e.tensor_scalar(out=sb["memKT"][:, hm, dc, :], in0=p[:, 0:256], scalar1=sb["gq16"][:, dc:dc + 1], scalar2=None, op0=ALU.mult))

    def tile(kind, ti):
        prompt = kind == "p"
        CR = 128 if prompt else SQ
        nt = 4 * CR
        NSEG, SLEN = (1, 512) if prompt else (SBN, SQ)
        t0 = ti * 512
        xt = sb["xt"]; xnT = sb["xnT"]; Vbf = sb["Vbf"]
        if prompt:
            tr.dma("sp", [], ["xt"], lambda e: e.dma_start(out=xt[:, :, :], in_=xp[t0:t0 + 512, :].rearrange("(c p) f -> p c f", p=128)), "xt")
        else:
            tr.dma("sp", [], ["xt"], lambda e: e.dma_start(out=xt[0:CR, :, :], in_=xs.rearrange("(c p) f -> p c f", p=CR)), "xt")
        rmsnorm_T("xt", xt, G_MIX, CR, 4, "xnT", xnT, nt)

        if KD == "t_norm":
            raise StopBuild()
        for which in range(2):
            for half in range(2):
                sn_, slab = load_slab(("in", which * 2 + half))
                for hh in range(4):
                    h = half * 4 + hh
                    pn, p = nps()
                    mm_fm(p, pn, sn_, slab, hh * 128, "xnT", xnT, nt)
                    if which == 0:
                        tr.op("act", [pn], ["QT"], lambda e: e.mul(out=QT[:, h, 0:nt], in_=p[:, 0:nt], mul=float(1.0 / np.sqrt(DH))))
                    else:
                        tr.op("act", [pn], ["KTn"], lambda e: e.copy(out=KTn[:, h, 0:nt], in_=p[:, 0:nt]))
                        fn, f = ntf()
                        tr.op("dve", [pn], [fn], lambda e: e.tensor_copy(out=f[:, 0:nt], in_=p[:, 0:nt]))
                        if prompt:
                            tr.dma(STQ, [fn], ["kp"], lambda e: e.dma_start(out=kp[h, :, t0:t0 + 512], in_=f[:, 0:512]), fn)
                        else:
                            tr.dma(STQ, [fn], ["ksn"], lambda e: e.dma_start(out=ksn[h, :, :], in_=f[:, 0:nt]), fn)
        if prompt and ti < NTILE - 1:
            tr.dma(STQ, ["KTn"], [f"kts{ti}"], lambda e: e.dma_start(out=ktscr[:, :, t0:t0 + 512].rearrange("h d t -> d h t"), in_=KTn[:, :, :]), "KTn")
        if KD == "t_qk":
            raise StopBuild()
        for half in range(2):
            sn_, slab = load_slab(("in", 4 + half))
            for c in range(4):
                pn, p = nps()
                for k in range(8):
                    tr.op("pe", [sn_, "xnT"], [pn], lambda e: e.matmul(p[0:CR, :], lhsT=xnT[:, k, c * CR:(c + 1) * CR], rhs=slab[:, k, :], start=(k == 0), stop=(k == 7)), signal=(k == 7))
                tr.op("act", [pn], ["Vbf"], lambda e: e.copy(out=Vbf[0:CR, c, half * 512:(half + 1) * 512], in_=p[0:CR, :]))
                fn, f = ntf()
                tr.op("dve", [pn], [fn], lambda e: e.tensor_copy(out=f[0:CR, 0:512], in_=p[0:CR, :]))
                if prompt:
                    tr.dma(STQ, [fn], ["vp"], lambda e: e.dma_start(out=vp[half * 4:half * 4 + 4, t0 + c * 128:t0 + (c + 1) * 128, :].rearrange("h t d -> t h d"), in_=f[:, 0:512].rearrange("p (h d) -> p h d", h=4)), fn)
                else:
                    tr.dma(STQ, [fn], ["vsn"], lambda e: e.dma_start(out=vsn[c, half * 4:half * 4 + 4, :, :].rearrange("h t d -> t h d"), in_=f[0:CR, 0:512].rearrange("p (h d) -> p h d", h=4)), fn)
        if prompt and ti < NTILE - 1:
            for c in range(4):
                tr.dma(STQ, ["Vbf"], [f"vs{ti}"], lambda e: e.dma_start(out=vscr[:, :, ti * 4 + c, :].rearrange("h p d -> p h d"), in_=Vbf[:, c, :].rearrange("p (h d) -> p h d", h=H)), "Vbf")

        if KD == "t_v":
            raise StopBuild()
        blocks = []
        if prompt:
            for h in range(H):
                q = QT[:, h, 0:512]
                first = True
                total = 4 * (ti + 1)
                cntb = 0

                def outfn(on, o, h=h):
                    tr.op("dve", [on], ["osbT"], lambda e: e.tensor_copy(out=sb["osbT"][:, h, 0:512], in_=o[:, 0:512]))

                for j in (3, 2, 1, 0):
                    cntb += 1
                    blocks.append(dict(q=QT[:, h, j * 128:512], nq=512 - j * 128, c0=j * 128, nk=128, kT=(lambda h=h, j=j: KTn[:, h, j * 128:(j + 1) * 128]),
                                       v=(lambda h=h, j=j: Vbf[:, j, h * 128:(h + 1) * 128]),
                                       mask=cst[:, C_MASK + j * 512 + j * 128:C_MASK + (j + 1) * 512], first=first, last=(cntb == total),
                                       reads=["QT", "KTn", "Vbf"], piece=None, out=outfn))
                    first = False
                tj = ti
                while tj > 0:
                    lo = max(0, tj - 2)
                    ntl = tj - lo
                    holder = {}

                    def issue(pc, h=h, lo=lo, ntl=ntl, holder=holder):
                        i = nxt("kr", NKR)
                        holder["i"] = i
                        kr, vr = sb[f"kr{i}"], sb[f"vr{i}"]
                        rd = [f"kts{t_}" for t_ in range(lo, lo + ntl)]
                        rv = [f"vs{t_}" for t_ in range(lo, lo + ntl)]
                        tr.dma("sp", rd, [f"kr{i}"], lambda e: e.dma_start(out=kr[:, 0:ntl * 512], in_=ktscr[h, :, lo * 512:(lo + ntl) * 512]), f"kr{i}")
                        tr.dma("sp", rv, [f"vr{i}"], lambda e: e.dma_start(out=vr[:, 0:ntl * 4, :], in_=vscr[h, :, lo * 4:(lo + ntl) * 4, :]), f"vr{i}")

                    pc = Piece(issue)
                    for jb in range(ntl * 4 - 1, -1, -1):
                        cntb += 1
                        blocks.append(dict(q=q, nq=512, nk=128,
                                           kT=(lambda holder=holder, jb=jb: sb[f"kr{holder['i']}"][:, jb * 128:(jb + 1) * 128]),
                                           v=(lambda holder=holder, jb=jb: sb[f"vr{holder['i']}"][:, jb, :]),
                                           mask=None, first=False, last=(cntb == total), reads=["QT"], piece=pc, out=outfn,
                                           holder=holder))
                    tj = lo
            for b in blocks:
                if b["piece"] is not None:
                    hd = b["holder"]
                    b["reads"] = _DynReads(hd)
        else:
            for s in range(SBN):
                for h in range(H):
                    q = QT[:, h, s * SQ:(s + 1) * SQ]

                    def outfn(on, o, h=h, s=s):
                        tr.op("dve", [on], ["osbT"], lambda e: e.tensor_copy(out=sb["osbT"][:, h, s * SQ:(s + 1) * SQ], in_=o[:, 0:SQ]))

                    blocks.append(dict(q=q, nq=SQ, nk=SQ, kT=(lambda h=h, s=s: KTn[:, h, s * SQ:(s + 1) * SQ]),
                                       v=(lambda h=h, s=s: Vbf[0:SQ, s, h * 128:(h + 1) * 128]),
                                       mask=cst[0:SQ, C_MASK:C_MASK + SQ], first=True, last=(NPB == 0),
                                       reads=["QT", "KTn", "Vbf"], piece=None, out=outfn))
                    holder = {}

                    def issue(pc, h=h, s=s, holder=holder):
                        i = nxt("kr", NKR)
                        holder["i"] = i
                        kr, vr = sb[f"kr{i}"], sb[f"vr{i}"]
                        tbn, tb_ = ntb()
                        tbn2, tb2 = ntb()
                        for half in range(2):
                            tgt = tb_ if half == 0 else tb2
                            tn_ = tbn if half == 0 else tbn2
                            nb = min(4, NPB - half * 4)
                            if nb <= 0:
                                continue
                            fn_, f_ = ntf()
                            tr.dma("sp", [], [fn_], lambda e: e.dma_start(out=f_[:, 0:nb * 128].rearrange("p (b d) -> p b d", d=128), in_=csk[s, h, half * 512:half * 512 + nb * 128, :].rearrange("(b p) d -> p b d", p=128)), fn_)
                            tr.op("dve", [fn_], [tn_], lambda e: e.tensor_copy(out=tgt[:, 0:nb * 128], in_=f_[:, 0:nb * 128]))
                            fn2_, f2_ = ntf()
                            tr.dma("sp", [], [fn2_], lambda e: e.dma_start(out=f2_[:, 0:nb * 128].rearrange("p (b d) -> p b d", d=128), in_=csv[s, h, half * 512:half * 512 + nb * 128, :].rearrange("(b p) d -> p b d", p=128)), fn2_)
                            tr.op("dve", [fn2_], [f"vr{i}"], lambda e: e.tensor_copy(out=vr[:, half * 4:half * 4 + nb, :], in_=f2_[:, 0:nb * 128].rearrange("p (b d) -> p b d", d=128)))
                        pn, p = npt()
                        for bI in range(NPB):
                            tgt = tb_ if bI < 4 else tb2
                            tn_ = tbn if bI < 4 else tbn2
                            tr.op("pe", [tn_, "cst"], [pn], lambda e: e.transpose(out=p[:, bI * 128:(bI + 1) * 128], in_=tgt[:, (bI % 4) * 128:(bI % 4 + 1) * 128], identity=ident), signal=(bI == NPB - 1))
                        tr.op("dve", [pn], [f"kr{i}"], lambda e: e.tensor_copy(out=kr[:, 0:NPB * 128], in_=p[:, 0:NPB * 128]))

                    pc = Piece(issue)
                    for jb in range(NPB - 1, -1, -1):
                        blocks.append(dict(q=q, nq=SQ, nk=128,
                                           kT=(lambda holder=holder, jb=jb: sb[f"kr{holder['i']}"][:, jb * 128:(jb + 1) * 128]),
                                           v=(lambda holder=holder, jb=jb: sb[f"vr{holder['i']}"][:, jb, :]),
                                           mask=None, first=False, last=(jb == 0), reads=_DynReads(holder), piece=pc, out=outfn, holder=holder))
        run_attention(blocks)

        if KD == "t_att":
            raise StopBuild()
        for half in range(2):
            sng, slg = load_slab(("in", 8 + half))
            for cc in range(4):
                pgn, pg = nps()
                mm_fm(pg, pgn, sng, slg, cc * 128, "xnT", xnT, nt)
                tr.op("act", [pgn], ["gel"], lambda e: e.activation(out=gel[:, half * 4 + cc, 0:nt], in_=pg[:, 0:nt], func=AF.Gelu_apprx_tanh))
        for half in range(2):
            snx, slx = load_slab(("in", 6 + half))
            for cc in range(4):
                c8 = half * 4 + cc
                xln, xl = ntf()
                xl3 = xl[:, 0:NSEG * (3 + SLEN)].rearrange("p (s l) -> p s l", s=NSEG)
                pn, p = nps()
                mm_fm(p, pn, snx, slx, cc * 128, "xnT", xnT, nt)
                if prompt:
                    tr.op("dve", ["hist"], [xln], lambda e: e.tensor_copy(out=xl3[:, 0, 0:3], in_=sb["hist"][:, c8, :]))
                else:
                    tr.op("dve", ["sconv"], [xln], lambda e: e.tensor_copy(out=xl3[:, :, 0:3], in_=sb["sconv"][:, c8, :, :]))
                tr.op("act", [pn], [xln], lambda e: e.copy(out=xl3[:, :, 3:3 + SLEN], in_=p[:, 0:nt].rearrange("p (s l) -> p s l", s=NSEG)))
                if prompt:
                    tr.op("dve", [xln], ["hist"], lambda e: e.tensor_copy(out=sb["hist"][:, c8, :], in_=xl3[:, 0, SLEN:SLEN + 3]))
                else:
                    tr.op("dve", [xln], ["csam"], lambda e: e.tensor_copy(out=sb["csam"][:, c8, :, :], in_=xl3[:, :, SLEN:SLEN + 3]))
                xcn, xc = ntf()
                xc3 = xc[:, 0:nt].rearrange("p (s l) -> p s l", s=NSEG)
                tr.op("dve", [xln, "pvec"], [xcn], lambda e: e.tensor_scalar(out=xc3, in0=xl3[:, :, 0:SLEN], scalar1=pvec[:, PV_CW + c8:PV_CW + c8 + 1], scalar2=pvec[:, PV_CB + c8:PV_CB + c8 + 1], op0=ALU.mult, op1=ALU.add))
                for i in range(1, 4):
                    tr.op("dve", [xln, "pvec", xcn], [xcn], lambda e: e.scalar_tensor_tensor(out=xc3, in0=xl3[:, :, i:i + SLEN], scalar=pvec[:, PV_CW + i * 8 + c8:PV_CW + i * 8 + c8 + 1], in1=xc3, op0=ALU.mult, op1=ALU.add))
                xbn, xcb = ntb()
                tr.op("dve", [xcn], [xbn], lambda e: e.tensor_copy(out=xcb[:, 0:nt], in_=xc[:, 0:nt]))
                prn, pr = nps()
                tr.op("pe", ["wa", xbn], [prn], lambda e: e.matmul(pr[:, 0:nt], lhsT=sb["wa"][:, c8, :], rhs=xcb[:, 0:nt], start=True, stop=True))
                pin, pi_ = nps()
                tr.op("pe", ["wx", xbn], [pin], lambda e: e.matmul(pi_[:, 0:nt], lhsT=sb["wx"][:, c8, :], rhs=xcb[:, 0:nt], start=True, stop=True))
                rn, r = ntf()
                tr.op("act", [prn, "pvec"], [rn], lambda e: e.activation(out=r[:, 0:nt], in_=pr[:, 0:nt], func=AF.Sigmoid, bias=pvec[:, PV_BA + c8:PV_BA + c8 + 1]))
                in_, ig = ntf()
                tr.op("act", [pin, "pvec"], [in_], lambda e: e.activation(out=ig[:, 0:nt], in_=pi_[:, 0:nt], func=AF.Sigmoid, bias=pvec[:, PV_BX + c8:PV_BX + c8 + 1]))
                an_, a = ntf()
                tr.op("act", [rn, "lrc0"], [an_], lambda e: e.activation(out=a[:, 0:nt], in_=r[:, 0:nt], func=AF.Exp, scale=lrc[:, 0, c8:c8 + 1]))
                a2n, a2 = ntf()
                tr.op("act", [rn, "lrc1"], [a2n], lambda e: e.activation(out=a2[:, 0:nt], in_=r[:, 0:nt], func=AF.Exp, scale=lrc[:, 1, c8:c8 + 1]))
                tr.op("act", [a2n], [a2n], lambda e: e.activation(out=a2[:, 0:nt], in_=a2[:, 0:nt], func=AF.Ln, scale=-1.0, bias=1.0))
                tr.op("act", [a2n], [a2n], lambda e: e.activation(out=a2[:, 0:nt], in_=a2[:, 0:nt], func=AF.Exp, scale=0.5))
                tr.op("dve", [in_, xcn], [in_], lambda e: e.tensor_tensor(out=ig[:, 0:nt], in0=ig[:, 0:nt], in1=xc[:, 0:nt], op=ALU.mult))
                tr.op("dve", [in_, a2n], [in_], lambda e: e.tensor_tensor(out=ig[:, 0:nt], in0=ig[:, 0:nt], in1=a2[:, 0:nt], op=ALU.mult))
                hn, hh_ = ntf()
                for sgi in range(NSEG):
                    init = sb["hcar"][:, c8:c8 + 1] if prompt else sb["slru"][:, c8, sgi:sgi + 1]
                    tr.op("dve", [an_, in_, "hcar", "slru"], [hn], lambda e: e.tensor_tensor_scan(out=hh_[:, sgi * SLEN:(sgi + 1) * SLEN], data0=a[:, sgi * SLEN:(sgi + 1) * SLEN], data1=ig[:, sgi * SLEN:(sgi + 1) * SLEN], initial=init, op0=ALU.mult, op1=ALU.add))
                if prompt:
                    tr.op("dve", [hn], ["hcar"], lambda e: e.tensor_copy(out=sb["hcar"][:, c8:c8 + 1], in_=hh_[:, nt - 1:nt]))
                else:
                    tr.op("dve", [hn], ["hsam"], lambda e: e.tensor_copy(out=sb["hsam"][:, c8, :], in_=hh_[:, 0:nt].rearrange("p (s l) -> p s l", s=NSEG)[:, :, SLEN - 1]))
                tr.op("dve", [hn, "gel"], ["olruT"], lambda e: e.tensor_tensor(out=sb["olruT"][:, c8, 0:nt], in0=hh_[:, 0:nt], in1=gel[:, c8, 0:nt], op=ALU.mult))

        if KD == "t_lru":
            raise StopBuild()
        nbat = 1 if prompt else SBN
        for half in range(2):
            snq, slq = load_slab(("in", 10 + half))
            for hh in range(2):
                hm = half * 2 + hh
                qfs = [("qf0", sb["qf0"]), ("qf1", sb["qf1"])]; sqs = [("sq0", sb["sq0"]), ("sq1", sb["sq1"])]
                for dc in range(2):
                    pn, p = nps()
                    mm_fm(p, pn, snq, slq, (hh * 2 + dc) * 128, "xnT", xnT, nt)
                    tr.op("act", [pn], [sqs[dc][0]], lambda e: e.activation(out=sqs[dc][1][:, 0:nt], in_=p[:, 0:nt], func=AF.Square))
                    tr.op("dve", [pn], [qfs[dc][0]], lambda e: e.tensor_copy(out=qfs[dc][1][:, 0:nt], in_=p[:, 0:nt]))
                pn, p = nps()
                for dc in range(2):
                    tr.op("pe", [sqs[dc][0], "cst"], [pn], lambda e: e.matmul(p[:, 0:nt], lhsT=ones, rhs=sqs[dc][1][:, 0:nt], start=(dc == 0), stop=(dc == 1)), signal=(dc == 1))
                rsn, rs = "rsq", sb["rsq"]
                tr.op("dve", [pn], [rsn], lambda e: e.tensor_scalar(out=rs[:, 0:nt], in0=p[:, 0:nt], scalar1=1.0 / MD, scalar2=EPS, op0=ALU.mult, op1=ALU.add))
                tr.op("act", [rsn], [rsn], lambda e: e.activation(out=rs[:, 0:nt], in_=rs[:, 0:nt], func=AF.Ln))
                tr.op("act", [rsn], [rsn], lambda e: e.activation(out=rs[:, 0:nt], in_=rs[:, 0:nt], func=AF.Exp, scale=-0.5))
                qns = []
                for dc in range(2):
                    qnn, qn = f"qn{dc}", sb[f"qn{dc}"]
                    tr.op("dve", [qfs[dc][0], rsn], [qnn], lambda e: e.tensor_tensor(out=qn[:, 0:nt], in0=qfs[dc][1][:, 0:nt], in1=rs[:, 0:nt], op=ALU.mult))
                    qns.append((qnn, qn))
                for bI in range(nbat):
                    if not prompt and hm == 0 and half == 0:
                        pass
                    c0, c1 = (0, nt) if prompt else (bI * SQ, (bI + 1) * SQ)
                    if not prompt:
                        load_sample_mem(bI, hm)
                    es = []
                    for ncI in range(2):
                        pn, p = nps()
                        for dc in range(2):
                            tr.op("pe", ["memKT", qns[dc][0]], [pn], lambda e: e.matmul(p[:, c0:c1], lhsT=sb["memKT"][:, hm if prompt else 0, dc, ncI * 128:(ncI + 1) * 128], rhs=qns[dc][1][:, c0:c1], start=(dc == 0), stop=(dc == 1)), signal=(dc == 1))
                        en, eb = ntb()
                        tr.op("act", [pn], [en], lambda e: e.activation(out=eb[:, c0:c1], in_=p[:, c0:c1], func=AF.Exp))
                        es.append((en, eb))
                    pn, p = nps()
                    for ncI in range(2):
                        tr.op("pe", [es[ncI][0], "cst"], [pn], lambda e: e.matmul(p[:, c0:c1], lhsT=ones, rhs=es[ncI][1][:, c0:c1], start=(ncI == 0), stop=(ncI == 1)), signal=(ncI == 1))
                    rcn, rc = ntf()
                    tr.op("dve", [pn], [rcn], lambda e: e.reciprocal(out=rc[:, c0:c1], in_=p[:, c0:c1]))
                    for dc in range(2):
                        pn, p = nps()
                        for ncI in range(2):
                            vcol = (hm if prompt else 0) * 256 + dc * 128
                            tr.op("pe", ["memV", es[ncI][0]], [pn], lambda e: e.matmul(p[:, c0:c1], lhsT=sb["memV"][:, ncI, vcol:vcol + 128], rhs=es[ncI][1][:, c0:c1], start=(ncI == 0), stop=(ncI == 1)), signal=(ncI == 1))
                        tr.op("dve", [pn, rcn], ["omemT"], lambda e: e.tensor_tensor(out=sb["omemT"][:, hm * 2 + dc, c0:c1], in0=p[:, c0:c1], in1=rc[:, c0:c1], op=ALU.mult))

        if KD == "t_mem":
            raise StopBuild()
        for jg in range(2):
            for b, (oname) in enumerate(["osbT", "olruT", "omemT"]):
                sng_, slg_ = load_slab(("in", 12 + b * 2 + jg))
                snb, slb = load_slab(("br", b, jg))
                oT = sb[oname]
                for jj in range(4):
                    j = jg * 4 + jj
                    pn, p = nps()
                    mm_fm(p, pn, sng_, slg_, jj * 128, "xnT", xnT, nt)
                    gn, gt = ntf()
                    bcol = PV_BM + b * 8 + j
                    tr.op("act", [pn, "pvec"], [gn], lambda e: e.activation(out=gt[:, 0:nt], in_=p[:, 0:nt], func=AF.Sigmoid, bias=pvec[:, bcol:bcol + 1]))
                    pn2, p2 = nps()
                    mm_fm(p2, pn2, snb, slb, jj * 128, oname, oT, nt)
                    an_ = f"acc{jj}"
                    accs = sb["accs"]
                    if b == 0:
                        tr.op("dve", [pn2, gn], [an_], lambda e: e.tensor_tensor(out=accs[:, jj, 0:nt], in0=p2[:, 0:nt], in1=gt[:, 0:nt], op=ALU.mult))
                    else:
                        tr.op("dve", [pn2, gn], [gn], lambda e: e.tensor_tensor(out=gt[:, 0:nt], in0=p2[:, 0:nt], in1=gt[:, 0:nt], op=ALU.mult))
                        if b == 1:
                            tr.op("dve", [gn, an_], [an_], lambda e: e.tensor_tensor(out=accs[:, jj, 0:nt], in0=accs[:, jj, 0:nt], in1=gt[:, 0:nt], op=ALU.add))
                        else:
                            tr.op("dve", [gn, an_], ["mergedT"], lambda e: e.tensor_tensor(out=sb["mergedT"][:, j, 0:nt], in0=accs[:, jj, 0:nt], in1=gt[:, 0:nt], op=ALU.add))

        if KD == "t_merge":
            raise StopBuild()
        mergedT = sb["mergedT"]
        for half in range(2):
            sno, slo = load_slab(("out", half))
            for c in range(4):
                pn, p = nps()
                for k in range(8):
                    tr.op("pe", [sno, "mergedT"], [pn], lambda e: e.matmul(p[0:CR, :], lhsT=mergedT[:, k, c * CR:(c + 1) * CR], rhs=slo[:, k, :], start=(k == 0), stop=(k == 7)), signal=(k == 7))
                tr.op("dve", [pn, "xt"], ["xt"], lambda e: e.tensor_tensor(out=xt[0:CR, c, half * 512:(half + 1) * 512], in0=xt[0:CR, c, half * 512:(half + 1) * 512], in1=p[0:CR, :], op=ALU.add))

        import os
        if os.environ.get("KDBG") == "noffn":
            if prompt:
                tr.dma(STQ, ["xt"], ["yp"], lambda e: e.dma_start(out=yp[t0:t0 + 512, :].rearrange("(c p) f -> p c f", p=128), in_=xt[:, :, :]), "xt")
            else:
                tr.dma(STQ, ["xt"], ["ys"], lambda e: e.dma_start(out=ys.rearrange("(c p) f -> p c f", p=CR), in_=xt[0:CR, :, :]), "xt")
            return
        rmsnorm_T("xt", xt, G_FFN, CR, 4, "xnT", xnT, nt)
        for j in range(6):
            sng_, slg_ = load_slab(("fg", j))
            snu, slu = load_slab(("fu", j))
            nf = 4 if j < 5 else 2
            for ff in range(nf):
                f = j * 4 + ff
                pn, p = nps()
                mm_fm(p, pn, sng_, slg_, ff * 128, "xnT", xnT, nt)
                pn2, p2 = nps()
                mm_fm(p2, pn2, snu, slu, ff * 128, "xnT", xnT, nt)
                sn_, s_ = ntf()
                tr.op("act", [pn], [sn_], lambda e: e.activation(out=s_[:, 0:nt], in_=p[:, 0:nt], func=AF.Silu))
                tr.op("dve", [pn2, sn_], ["hT"], lambda e: e.tensor_tensor(out=hT[:, f, 0:nt], in0=p2[:, 0:nt], in1=s_[:, 0:nt], op=ALU.mult))
        for half in range(2):
            pss = [nps() for _ in range(4)]
            for kg in range(3):
                snd, sld = load_slab(("fd", half, kg))
                kc = SL[("fd", half, kg)][3]
                for c in range(4):
                    pn, p = pss[c]
                    for kk in range(kc):
                        f = kg * 8 + kk
                        last = (kg == 2 and kk == kc - 1)
                        tr.op("pe", [snd, "hT"], [pn], lambda e: e.matmul(p[0:CR, :], lhsT=hT[:, f, c * CR:(c + 1) * CR], rhs=sld[:, kk, :], start=(f == 0), stop=last), signal=(kk == kc - 1))
            for c in range(4):
                pn, p = pss[c]
                tr.op("dve", [pn, "xt"], ["xt"], lambda e: e.tensor_tensor(out=xt[0:CR, c, half * 512:(half + 1) * 512], in0=xt[0:CR, c, half * 512:(half + 1) * 512], in1=p[0:CR, :], op=ALU.add))
        if prompt:
            tr.dma(STQ, ["xt"], ["yp"], lambda e: e.dma_start(out=yp[t0:t0 + 512, :].rearrange("(c p) f -> p c f", p=128), in_=xt[:, :, :]), "xt")
        else:
            tr.dma(STQ, ["xt"], ["ys"], lambda e: e.dma_start(out=ys.rearrange("(c p) f -> p c f", p=CR), in_=xt[0:CR, :, :]), "xt")

    class _DynReads(list):
        def __init__(self, holder):
            super().__init__()
            self.holder = holder

        def __iter__(self):
            i = self.holder["i"]
            return iter(["QT", f"kr{i}", f"vr{i}"])

        def __add__(self, other):
            return list(self) + list(other)

        def __radd__(self, other):
            return list(other) + list(self)

    loaded_mem = {}

    def load_sample_mem(bI, hm):
        tbn, tb_ = ntb()
        fn_, f_ = ntf()
        tr.dma("sp", [], [fn_], lambda e: e.dma_start(out=f_[:, 0:512].rearrange("p (c d) -> p c d", c=2), in_=cmk[bI, hm, :, :].rearrange("(c p) d -> p c d", p=128)), fn_)
        tr.op("dve", [fn_], [tbn], lambda e: e.tensor_copy(out=tb_[:, :], in_=f_[:, 0:512]))
        fn2_, f2_ = ntf()
        tr.dma("sp", [], [fn2_], lambda e: e.dma_start(out=f2_[:, 0:512].rearrange("p (c d) -> p c d", c=2), in_=cmv[bI, hm, :, :].rearrange("(c p) d -> p c d", p=128)), fn2_)
        tr.op("dve", [fn2_], ["memV"], lambda e: e.tensor_copy(out=sb["memV"][:, :, 0:256], in_=f2_[:, 0:512].rearrange("p (c d) -> p c d", c=2)))
        for dc in range(2):
            pn, p = npt()
            for c in range(2):
                tr.op("pe", [tbn, "cst"], [pn], lambda e: e.transpose(out=p[:, c * 128:(c + 1) * 128], in_=tb_[:, c * 256 + dc * 128:c * 256 + (dc + 1) * 128], identity=ident), signal=(c == 1))
            tr.op("dve", [pn, "gq16"], ["memKT"], lambda e: e.tensor_scalar(out=sb["memKT"][:, 0, dc, :], in0=p[:, 0:256], scalar1=sb["gq16"][:, dc:dc + 1], scalar2=None, op0=ALU.mult))

    salloc("accs", [128, 4, 512], F32)

    per_tile = [("in", j) for j in range(6)] + [("in", 8), ("in", 9), ("in", 6), ("in", 7), ("in", 10), ("in", 11)]
    for jg in range(2):
        for b in range(3):
            per_tile += [("in", 12 + b * 2 + jg), ("br", b, jg)]
    per_tile += [("out", 0), ("out", 1)]
    for j in range(6):
        per_tile += [("fg", j), ("fu", j)]
    for half in range(2):
        for kg in range(3):
            per_tile.append(("fd", half, kg))
    sched["list"] = per_tile * (NTILE + 1)
    if KD == "setup":
        tr.final_wait("sp")
        return nc
    mem_prompt_prep()
    if KD.startswith("mem"):
        tr.final_wait("sp")
        return nc
    try:
        for ti in range(NTILE):
            tile("p", ti)
    except StopBuild:
        tr.final_wait("sp")
        return nc
    tr.dma(STQ, ["hist"], ["cp"], lambda e: e.dma_start(out=cp, in_=sb["hist"][:]), "hist")
    tr.dma(STQ, ["hcar"], ["hp"], lambda e: e.dma_start(out=hp, in_=sb["hcar"][:]), "hcar")
    tile("s", 0)
    tr.dma(STQ, ["csam"], ["cs"], lambda e: e.dma_start(out=cs, in_=sb["csam"][:]), "csam")
    tr.dma(STQ, ["hsam"], ["hs"], lambda e: e.dma_start(out=hs, in_=sb["hsam"][:]), "hsam")
    tr.final_wait("sp")
    return nc


def _consts():
    c = np.zeros((128, NCST), np.float32)
    p = np.arange(128)[:, None]
    j = np.arange(128)[None, :]
    c[:, C_ID:C_ID + 128] = (p == j)
    c[:, C_NTRI:C_NTRI + 128] = -1.0 * (p >= j)
    c[:, C_NONE:C_NONE + 128] = -1.0
    c[:, C_ONE:C_ONE + 128] = 1.0
    t = np.arange(512)[None, :]
    for jb in range(4):
        c[:, C_MASK + jb * 512:C_MASK + (jb + 1) * 512] = (t > 128 * jb + p)
    return c


def _fm(v):
    v = np.asarray(v, np.float32)
    lead = v.shape[:-1]
    c = v.shape[-1] // 128
    v = v.reshape(lead + (c, 128))
    return np.ascontiguousarray(np.moveaxis(v, -1, 0))


_CACHE = {}


def run(inputs, T, PAST, SBN, SQ, ncores):
    key = (T, PAST, SBN, SQ)
    if key not in _CACHE:
        _CACHE[key] = build(T, PAST, SBN, SQ)
    nc = _CACHE[key]
    f = lambda a: np.ascontiguousarray(np.asarray(a, np.float32))
    pv = np.zeros((128, NPV), np.float32)
    pv[:, PV_BM:PV_BM + 24] = _fm(inputs["b_merge"][0])
    pv[:, PV_CW:PV_CW + 32] = _fm(inputs["conv_w"][0]).reshape(128, 32)
    pv[:, PV_CB:PV_CB + 8] = _fm(inputs["conv_b"][0])
    pv[:, PV_BA:PV_BA + 8] = _fm(inputs["lru_ba"][0])
    pv[:, PV_BX:PV_BX + 8] = _fm(inputs["lru_bx"][0])
    pv[:, PV_AL:PV_AL + 8] = _fm(inputs["lru_a_logit"][0])
    pv[:, PV_GQ:PV_GQ + 2] = _fm(inputs["q_norm_g"][0])
    bv = np.concatenate([f(inputs["norm_mix_g"][0]), f(inputs["norm_ffn_g"][0]), f(inputs["mem_norm_g"][0]), f(inputs["k_norm_g"][0])])[None, :]
    cst = _consts()
    shared = {"pvec": pv, "bvec": np.ascontiguousarray(bv), "cst": cst,
              "lru_wa": f(inputs["lru_wa"][0]), "lru_wx": f(inputs["lru_wx"][0])}
    for w in ["w_in", "w_mem_k", "w_mem_v", "w_br_sb", "w_br_lru", "w_br_mem", "w_out", "w_ffn_gate", "w_ffn_up", "w_ffn_down"]:
        shared[w] = f(inputs[w][0])
    in_maps = []
    for c in range(ncores):
        sl = slice(c * SBN, (c + 1) * SBN)
        m = dict(shared)
        m["xp"] = f(inputs["x_prompt"][c])
        m["xs"] = f(inputs["x_sample"][sl]).reshape(SBN * SQ, D)
        m["memp"] = f(inputs["mem_prompt"][c])
        m["csk"] = f(inputs["cache_sb_k"][0, sl]); m["csv"] = f(inputs["cache_sb_v"][0, sl])
        m["sconv"] = np.ascontiguousarray(_fm(inputs["state_conv"][0, sl]))
        m["sconv"] = np.ascontiguousarray(m["sconv"].transpose(0, 3, 1, 2))
        m["slru"] = np.ascontiguousarray(_fm(inputs["state_lru_h"][0, sl]).transpose(0, 2, 1))
        m["cmk"] = f(inputs["cache_mem_k"][0, sl]); m["cmv"] = f(inputs["cache_mem_v"][0, sl])
        in_maps.append(m)
    res = run_bass_kernel_spmd(nc, in_maps, core_ids=list(range(ncores)))
    R = res.results

    def unfm(a):
        a = np.asarray(a)
        return np.moveaxis(a, (0, 1), (-1, -2)).reshape(a.shape[2:] + (1024,))

    yp = np.stack([R[c]["yp"] for c in range(ncores)])
    ys = np.concatenate([np.asarray(R[c]["ys"]).reshape(SBN, SQ, D) for c in range(ncores)])
    kp = np.stack([np.asarray(R[c]["kp"]).transpose(0, 2, 1) for c in range(ncores)])[None]
    vp = np.stack([R[c]["vp"] for c in range(ncores)])[None]
    cpo = np.stack([unfm(R[c]["cp"]) for c in range(ncores)])[None]
    hpo = np.stack([unfm(R[c]["hp"]) for c in range(ncores)])[None]
    mkp = np.stack([R[c]["mkp"] for c in range(ncores)])[None]
    mvp = np.stack([R[c]["mvp"] for c in range(ncores)])[None]
    ksn = np.concatenate([np.asarray(R[c]["ksn"]).reshape(H, DH, SBN, SQ).transpose(2, 0, 3, 1) for c in range(ncores)])[None]
    vsn = np.concatenate([R[c]["vsn"] for c in range(ncores)])[None]
    cso = np.concatenate([unfm(R[c]["cs"]) for c in range(ncores)])[None]
    hso = np.concatenate([unfm(R[c]["hs"]) for c in range(ncores)])[None]
    outs = (yp, ys, kp, vp, cpo, hpo, mkp, mvp, ksn, vsn, cso, hso)
    return tuple(np.ascontiguousarray(o, dtype=np.float32) for o in outs)


def kernel(**inputs):
    return run(inputs, 8192, 1024, 4, 64, NCORES)
```
